# Optimizing a Trainium2 kernel written in Bass

```python
import jax, jax.numpy as jnp
from jax import lax
import numpy as np

D_MODEL = 1024
BATCH = 8
SEQ = 2048
DEPTH = 4
DEC_BATCH = 128
DEC_SEQ = 8
PAST_LEN = 16384
PAGE_SIZE = 128

D_MIX = D_MODEL
HEAD_DIM = 64
D_A = 3 * D_MIX // 8
D_B = 3 * D_MIX // 8
D_C = D_MIX - D_A - D_B
N_HEADS_B = D_B // HEAD_DIM
CONV_A_WIDTH = 3
CONV_C_WIDTH = 31
CHUNK = 128
D_PLE = 256
D_FF = -(-8 * D_MODEL // (3 * 256)) * 256
D_IN = 3 * D_A + 2 * D_B + 2 * D_C
SPLITS = [D_A, 2 * D_A, 3 * D_A, 3 * D_A + D_B, 3 * D_A + 2 * D_B]
EPS = 1e-6

kernel_name = 'hybrid_shortconv_chunkmlp_conformer_step'


def rmsnorm(x, g):
    xf = x.astype(jnp.float32)
    y = xf * lax.rsqrt(jnp.mean(xf * xf, axis=-1, keepdims=True) + EPS)
    return (y * g.astype(jnp.float32)).astype(x.dtype)


def layernorm(x, g, b):
    xf = x.astype(jnp.float32)
    mu = jnp.mean(xf, axis=-1, keepdims=True)
    xc = xf - mu
    y = xc * lax.rsqrt(jnp.mean(xc * xc, axis=-1, keepdims=True) + EPS)
    return (y * g.astype(jnp.float32) + b.astype(jnp.float32)).astype(x.dtype)


def causal_dwconv(x, hist, w):
    xx = jnp.concatenate([hist.astype(x.dtype), x], axis=1)
    y = lax.conv_general_dilated(xx, w.astype(x.dtype)[:, None, :], (1,), 'VALID',
                                 dimension_numbers=('NWC', 'WIO', 'NWC'),
                                 feature_group_count=x.shape[-1])
    return y, xx[:, -(w.shape[0] - 1):]


def chunk_spatial_mix(v, w_s, b_s):
    n, L, hb, dh = v.shape
    nc = -(-L // CHUNK)
    vp = jnp.pad(v, ((0, 0), (0, nc * CHUNK - L), (0, 0), (0, 0))).reshape(n, nc, CHUNK, hb, dh)
    causal = jnp.tril(jnp.ones((CHUNK, CHUNK), dtype=bool))
    w = jnp.where(causal[None], w_s, 0).astype(v.dtype)
    s = jnp.einsum('hts,ncshd->ncthd', w, vp) + b_s.T.astype(v.dtype)[None, None, :, :, None]
    return s.reshape(n, nc * CHUNK, hb, dh)[:, :L]


def mixer_layer(h, p_i, hist_a, hist_c, g_mix, w_in, conv_a_w, ln_b_g, ln_b_b, w_s, b_s,
                conv_c_w, conv_c_b, ln_c_g, ln_c_b, g_out_a, g_out_b, g_out_c, w_out,
                g_ffn, w_gate, w_up, w_down, g_ple, w_ple_gate, w_ple_proj):
    n, L, _ = h.shape
    z = rmsnorm(h, g_mix) @ w_in
    xa, gate_b, gate_c, u, v, c_in = jnp.split(z, SPLITS, axis=-1)
    conv_a, new_a = causal_dwconv(gate_c * xa, hist_a, conv_a_w)
    y_a = gate_b * conv_a
    u = jax.nn.gelu(u, approximate=False)
    v = layernorm(jax.nn.gelu(v, approximate=False), ln_b_g, ln_b_b)
    s = chunk_spatial_mix(v.reshape(n, L, N_HEADS_B, HEAD_DIM), w_s, b_s).reshape(n, L, D_B)
    y_b = u * s
    new_v = v[:, L - ((L - 1) % CHUNK + 1):]
    glu = c_in[..., :D_C] * jax.nn.sigmoid(c_in[..., D_C:])
    conv_c, new_c = causal_dwconv(glu, hist_c, conv_c_w)
    y_c = jax.nn.silu(layernorm(conv_c + conv_c_b.astype(h.dtype), ln_c_g, ln_c_b))
    mixed = jnp.concatenate([rmsnorm(y_a, g_out_a), rmsnorm(y_b, g_out_b), rmsnorm(y_c, g_out_c)], axis=-1)
    h = h + mixed @ w_out
    f = rmsnorm(h, g_ffn)
    h = h + (jax.nn.silu(f @ w_gate) * (f @ w_up)) @ w_down
    gate = jax.nn.sigmoid(rmsnorm(h, g_ple) @ w_ple_gate)
    h = h + gate * (p_i @ w_ple_proj)
    return h, new_a, new_c, new_v


def run_trunk(x, p, hist_a, hist_c, layer_weights, g_final):
    h = x
    sa, sc, sv = [], [], []
    for i in range(DEPTH):
        h, na, nc, nv = mixer_layer(h, p[i], hist_a[i], hist_c[i], *[w[i] for w in layer_weights])
        sa.append(na)
        sc.append(nc)
        sv.append(nv)
    return rmsnorm(h, g_final), jnp.stack(sa), jnp.stack(sc), jnp.stack(sv)


def setup_inputs(seed: int = 0) -> dict:
    key = jax.random.key(seed)
    ks = jax.random.split(key, 32)
    f32 = jnp.float32

    def nrm(k, shape, scale):
        return jax.random.normal(k, shape, f32) * scale

    def gain(k, shape):
        return 1.0 + 0.1 * jax.random.normal(k, shape, f32)

    return {
        'x_prompt': nrm(ks[0], (BATCH, SEQ, D_MODEL), 1.0),
        'x_sample': nrm(ks[1], (DEC_BATCH, DEC_SEQ, D_MODEL), 1.0),
        'state_conv_a': nrm(ks[2], (DEPTH, DEC_BATCH, CONV_A_WIDTH - 1, D_A), 0.5),
        'state_conv_c': nrm(ks[3], (DEPTH, DEC_BATCH, CONV_C_WIDTH - 1, D_C), 0.5),
        'p_prompt': nrm(ks[4], (DEPTH, BATCH, SEQ, D_PLE), 1.0),
        'p_sample': nrm(ks[5], (DEPTH, DEC_BATCH, DEC_SEQ, D_PLE), 1.0),
        'g_mix': gain(ks[6], (DEPTH, D_MODEL)),
        'w_in': nrm(ks[7], (DEPTH, D_MODEL, D_IN), D_MODEL ** -0.5),
        'conv_a_w': nrm(ks[8], (DEPTH, CONV_A_WIDTH, D_A), CONV_A_WIDTH ** -0.5),
        'ln_b_g': gain(ks[9], (DEPTH, D_B)),
        'ln_b_b': nrm(ks[10], (DEPTH, D_B), 0.02),
        'w_s': nrm(ks[11], (DEPTH, N_HEADS_B, CHUNK, CHUNK), CHUNK ** -0.5),
        'b_s': gain(ks[12], (DEPTH, N_HEADS_B, CHUNK)),
        'conv_c_w': nrm(ks[13], (DEPTH, CONV_C_WIDTH, D_C), CONV_C_WIDTH ** -0.5),
        'conv_c_b': nrm(ks[14], (DEPTH, D_C), 0.02),
        'ln_c_g': gain(ks[15], (DEPTH, D_C)),
        'ln_c_b': nrm(ks[16], (DEPTH, D_C), 0.02),
        'g_out_a': gain(ks[17], (DEPTH, D_A)),
        'g_out_b': gain(ks[18], (DEPTH, D_B)),
        'g_out_c': gain(ks[19], (DEPTH, D_C)),
        'w_out': nrm(ks[20], (DEPTH, D_MIX, D_MODEL), D_MIX ** -0.5),
        'g_ffn': gain(ks[21], (DEPTH, D_MODEL)),
        'w_gate': nrm(ks[22], (DEPTH, D_MODEL, D_FF), D_MODEL ** -0.5),
        'w_up': nrm(ks[23], (DEPTH, D_MODEL, D_FF), D_MODEL ** -0.5),
        'w_down': nrm(ks[24], (DEPTH, D_FF, D_MODEL), D_FF ** -0.5),
        'g_ple': gain(ks[25], (DEPTH, D_MODEL)),
        'w_ple_gate': nrm(ks[26], (DEPTH, D_MODEL, D_MODEL), D_MODEL ** -0.5),
        'w_ple_proj': nrm(ks[27], (DEPTH, D_PLE, D_MODEL), D_PLE ** -0.5),
        'g_final': gain(ks[28], (D_MODEL,)),
    }


def reference(x_prompt, x_sample, state_conv_a, state_conv_c, p_prompt, p_sample,
              g_mix, w_in, conv_a_w, ln_b_g, ln_b_b, w_s, b_s, conv_c_w, conv_c_b,
              ln_c_g, ln_c_b, g_out_a, g_out_b, g_out_c, w_out, g_ffn, w_gate, w_up,
              w_down, g_ple, w_ple_gate, w_ple_proj, g_final):
    layer_weights = (g_mix, w_in, conv_a_w, ln_b_g, ln_b_b, w_s, b_s, conv_c_w, conv_c_b,
                     ln_c_g, ln_c_b, g_out_a, g_out_b, g_out_c, w_out, g_ffn, w_gate, w_up,
                     w_down, g_ple, w_ple_gate, w_ple_proj)
    nb = x_prompt.shape[0]
    hist_a0 = jnp.zeros((DEPTH, nb, CONV_A_WIDTH - 1, D_A), x_prompt.dtype)
    hist_c0 = jnp.zeros((DEPTH, nb, CONV_C_WIDTH - 1, D_C), x_prompt.dtype)
    y_prompt, conv_a_p, conv_c_p, chunk_v_p = run_trunk(x_prompt, p_prompt, hist_a0, hist_c0,
                                                        layer_weights, g_final)
    y_sample, conv_a_s, conv_c_s, chunk_v_s = run_trunk(x_sample, p_sample, state_conv_a, state_conv_c,
                                                        layer_weights, g_final)
    return (y_prompt, y_sample, conv_a_p, conv_a_s, conv_c_p, conv_c_s, chunk_v_p, chunk_v_s)
```

```python
import numpy as np
from contextlib import ExitStack
from collections import deque
import concourse.bass as bass
import concourse.mybir as mybir
from concourse.bass_utils import run_bass_kernel_spmd

F32 = mybir.dt.float32
BF16 = mybir.dt.bfloat16
ALU = mybir.AluOpType
AF = mybir.ActivationFunctionType
AX = mybir.AxisListType

DEPTH = 4
D = 1024
DIN = 2432
DFF = 2816
NCORE = 8
EPS = 1e-6
TC = 1152
XS0 = 1026
GS0 = 1054
XAW = 1186
GW = 1662
NS = 5
SLOTW = 3072
NVL = 109
NV = NVL * DEPTH + 8
ENGS = ("pe", "act", "dve", "pool", "sp")
NDS = 8


class _Stop(Exception):
    pass


_DBG = None


class Sched:
    def __init__(self):
        self.ops = {e: [] for e in ENGS}
        self.cnt = {}
        self.known = {e: {} for e in ENGS}
        self.lastw = {}
        self.readers = {}
        self.stopped = False

    def op(self, eng, fn, reads=(), writes=(), fence=(), sem=None, inc=1, serialize=True):
        if self.stopped:
            return 0
        waits = {}

        def need(sv):
            if sv is not None and sv[1] > waits.get(sv[0], 0):
                waits[sv[0]] = sv[1]
        for k in reads:
            need(self.lastw.get(k))
        for k in list(writes) + list(fence):
            if serialize:
                need(self.lastw.get(k))
            for s, v in self.readers.get(k, {}).items():
                need((s, v))
        semname = sem or eng
        if sem is not None and serialize:
            need((semname, self.cnt.get(semname, 0)))
        kn = self.known[eng]
        wl = []
        for s, v in waits.items():
            if v <= 0 or (eng == "pe" and s == "pe"):
                continue
            if kn.get(s, 0) >= v:
                continue
            assert v <= self.cnt.get(s, 0), ("wait on a pending (unmaterialised) count", eng, s, v)
            kn[s] = v
            wl.append((s, v))
        if inc:
            self.cnt[semname] = self.cnt.get(semname, 0) + inc
            val = self.cnt[semname]
        else:
            val = self.cnt.get(semname, 0) + 1
        self.ops[eng].append((wl, fn, semname, inc))
        for k in writes:
            self.lastw[k] = (semname, val)
            self.readers[k] = {}
        for k in reads:
            self.readers.setdefault(k, {})[semname] = val
        return val


def build_program():
    nc = bass.Bass("TRN2", target_bir_lowering=False)

    def din(name, shape):
        return nc.dram_tensor(name, list(shape), F32, kind="ExternalInput").ap()

    def dout(name, shape):
        return nc.dram_tensor(name, list(shape), F32, kind="ExternalOutput").ap()

    xin = din("xin", [2176, D])
    pin = din("pin", [DEPTH, 2176, 256])
    sa = din("sa", [DEPTH, 32, 384])
    sc = din("sc", [DEPTH, 480, 256])
    vec = din("vec", [128, NV])
    lnb = din("lnb", [DEPTH, 128, 2, 384])
    wsT = din("wsT", [DEPTH, 128, 6, 128])
    wsA = din("wsA", [DEPTH, 128, 6, 8])
    bsP = din("bsP", [DEPTH, 128, 3, 128])
    bsS = din("bsS", [DEPTH, 128, 3, 128])
    w_in = din("w_in", [DEPTH, D, DIN])
    w_out = din("w_out", [DEPTH, D, D])
    w_gate = din("w_gate", [DEPTH, D, DFF])
    w_up = din("w_up", [DEPTH, D, DFF])
    w_down = din("w_down", [DEPTH, DFF, D])
    w_pg = din("w_ple_gate", [DEPTH, D, D])
    w_pp = din("w_ple_proj", [DEPTH, 256, D])

    y_o = dout("y", [2176, D])
    nap_o = dout("nap", [DEPTH, 2, 384])
    nas_o = dout("nas", [DEPTH, 32, 384])
    ncp_o = dout("ncp", [DEPTH, 30, 256])
    ncs_o = dout("ncs", [DEPTH, 480, 256])
    nvp_o = dout("nvp", [DEPTH, 128, 384])
    nvs_o = dout("nvs", [DEPTH, 128, 384])

    S = Sched()
    es = ExitStack()
    with es:
        def sb(name, shape, dt):
            return es.enter_context(nc.sbuf_tensor(name, list(shape), dt))

        H = sb("H", [128, 8, TC], F32)
        NRM = sb("NRM", [128, 8, TC], BF16)
        Y = sb("Y", [128, 8, TC], F32)
        XA = sb("XA", [128, 2, XAW], F32)
        XAH = sb("XAH", [128, 3, 32], F32)
        GB = sb("GB", [128, 2, GW], BF16)
        GFS = sb("GFS", [128, 2, 608], F32)
        GFT = sb("GFT", [128, 2, 30], F32)
        DG = sb("DG", [128, 31, 128], BF16)
        VN = sb("VN", [128, 9, 384], BF16)
        VNF = sb("VNF", [128, 384], F32)
        VG = sb("VG", [128, 3, 384], F32)
        BST = sb("BST", [128, 3, 8], F32)
        SLOT = sb("SLOT", [128, NS, SLOTW], BF16)
        PT = sb("PT", [128, 2, TC], BF16)
        STG = sb("STG", [128, 2, 1024], F32)
        SQ = sb("SQ", [128, 4, 512], BF16)
        RB = sb("RB", [128, 3, 512], F32)
        TMP = sb("TMP", [128, 3, 512], F32)
        VEC = sb("VEC", [128, NV], F32)
        LNB = sb("LNB", [128, 2, 384], F32)
        WSA = sb("WSA", [128, 6, 8], F32)
        MP = sb("MP", [128, 6, 128], BF16)
        MS = sb("MS", [128, 6, 128], BF16)
        BSP = sb("BSP", [128, 3, 128], F32)
        BSS = sb("BSS", [128, 3, 128], F32)
        IDF = sb("IDF", [128, 128], F32)
        ONES = sb("ONES", [128, 128], BF16)
        DM = sb("DM", [128, 8, 16], F32)
        EQ = sb("EQ", [128, 16], F32)
        EPSC = sb("EPSC", [128, 1], F32)
        NEGH = sb("NEGH", [128, 1], F32)
        PSTG = sb("PSTG", [128, 4, 256], F32)
        CARA = sb("CARA", [128, DEPTH, 3, 2], F32)
        CARG = sb("CARG", [128, DEPTH, 2, 30], F32)
        PS = es.enter_context(nc.psum_tensor("PS", [128, 8, 512], F32))
        ACTB = Y[:].bitcast(BF16)[:, 0:4, :].rearrange("p a (b c) -> p (a b) c", c=TC)

        semnames = list(ENGS) + [f"d{i}" for i in range(NDS)] + [f"w{i}" for i in range(NS)]
        SEM = {n: es.enter_context(nc.semaphore("s_" + n)) for n in semnames}

        st = {"bank": 0, "sbank": 0, "ds": 0, "stg": 0, "sq": 0, "rb": 0, "tmp": 0, "pstg": 0, "pumpk": 3}

        def newbank():
            b = st["bank"]
            st["bank"] = (b + 1) % 6
            return b

        def statbank():
            b = st["sbank"]
            st["sbank"] = (b + 1) % 2
            return 6 + b

        bgq = deque()
        bg_done = set()
        bg_ctr = [0]

        def bg_add(steps):
            hid = bg_ctr[0]
            bg_ctr[0] += 1
            for i, fn in enumerate(steps):
                bgq.append((hid, fn, i == len(steps) - 1))
            return hid

        def pump(k=1):
            for _ in range(k):
                if not bgq:
                    return
                hid, fn, last = bgq.popleft()
                fn()
                if last:
                    bg_done.add(hid)

        def ensure(hid):
            while hid is not None and hid not in bg_done:
                assert bgq
                pump(1)

        def flush():
            while bgq:
                pump(1)

        def ring(name, n):
            i = st[name]
            st[name] = (i + 1) % n
            return i

        def dma(out, in_, reads=(), writes=(), fence=()):
            i = ring("ds", NDS)
            S.op("sp", lambda e: e.dma_start(out=out, in_=in_), reads=reads, writes=writes,
                 fence=fence, sem=f"d{i}", inc=16)

        def vcol(l, off, j=0):
            c = l * NVL + off + j
            return VEC[:, c:c + 1]

        plan = []

        def wplan():
            for hf in range(2):
                for l in range(DEPTH):
                    wi = w_in[l].rearrange("(kc p) m -> p kc m", p=128)
                    plan.append(("U", [(0, wi[:, :, 1152:1536])], 384))
                    plan.append(("V", [(0, wi[:, :, 1536:1920])], 384))
                    plan.append(("CV", [(0, wi[:, :, 1920:2176])], 256))
                    plan.append(("CG", [(0, wi[:, :, 2176:2432])], 256))
                    for j in range(3):
                        plan.append(("A", [(0, wi[:, :, j * 128:(j + 1) * 128]),
                                           (128, wi[:, :, 768 + j * 128:768 + (j + 1) * 128]),
                                           (256, wi[:, :, 384 + j * 128:384 + (j + 1) * 128])], 384))
                    wo = w_out[l].rearrange("(kc p) m -> p kc m", p=128)
                    for (m0, mw) in ((0, 384), (384, 384), (768, 256)):
                        plan.append(("O", [(0, wo[:, :, m0:m0 + mw])], mw))
                    wg = w_gate[l].rearrange("(kc p) m -> p kc m", p=128)
                    wu = w_up[l].rearrange("(kc p) m -> p kc m", p=128)
                    wd = w_down[l].rearrange("(kc p) m -> p kc m", p=128)
                    for pc in range(3):
                        j0 = (0, 8, 15)[pc]
                        nj = (8, 7, 7)[pc]
                        for (a, w_) in ((0, 3), (3, 3), (6, nj - 6)):
                            c0 = (j0 + a) * 128
                            plan.append(("FG", [(0, wg[:, :, c0:c0 + w_ * 128])], w_ * 128))
                            plan.append(("FU", [(0, wu[:, :, c0:c0 + w_ * 128])], w_ * 128))
                        for (a, w_) in ((0, 3), (3, 3), (6, nj - 6)):
                            plan.append(("FD", [(0, wd[:, j0 + a:j0 + a + w_, :])], 1024))
                    wpg = w_pg[l].rearrange("(kc p) m -> p kc m", p=128)
                    for ip_, (m0, mw) in enumerate(((0, 384), (384, 384), (768, 256))):
                        plan.append(("PG", [(0, wpg[:, :, m0:m0 + mw])], mw))
                        if ip_ == 0:
                            plan.append(("PP", [(0, w_pp[l].rearrange("(kc p) m -> p kc m", p=128))], 1024))
        wplan()
        wst = {"next_load": 0, "next_acq": 0, "released": 0}

        def slot_view(si, nk, w):
            return SLOT[:, si, 0:nk * w].rearrange("p (k m) -> p k m", k=nk)

        def try_load():
            while wst["next_load"] < len(plan) and wst["next_load"] - NS < wst["released"]:
                i = wst["next_load"]
                tag, parts, w = plan[i]
                si = i % NS
                for pi_, (off, src) in enumerate(parts):
                    nk = src.shape[1]
                    dst = slot_view(si, nk, w)[:, :, off:off + src.shape[2]]
                    S.op("pool", lambda e, dst=dst, src=src: e.dma_start(out=dst, in_=src),
                         writes=[("slot", si)], sem=f"w{si}", inc=16, serialize=(pi_ == 0))
                wst["next_load"] += 1

        def wacq(tag):
            i = wst["next_acq"]
            assert plan[i][0] == tag, (plan[i][0], tag)
            wst["next_acq"] += 1
            try_load()
            assert wst["next_load"] > i
            si = i % NS
            nk = plan[i][1][0][1].shape[1]
            return si, slot_view(si, nk, plan[i][2])

        def wrel(n=1):
            wst["released"] += n
            try_load()

        dma(VEC[:], vec, writes=[("VEC",)])
        S.op("pool", lambda e: e.memset(ONES[:], 1.0), writes=[("ONES",)])
        S.op("pool", lambda e: e.memset(EPSC[:], EPS), writes=[("EPSC",)])
        S.op("pool", lambda e: e.memset(NEGH[:], -0.5), writes=[("NEGH",)])
        S.op("pool", lambda e: e.memset(IDF[:], 1.0), writes=[("IDF",)])
        S.op("pool", lambda e: e.affine_select(IDF[:], IDF[:], [[-1, 128]], ALU.is_equal, 0.0,
                                               base=0, channel_multiplier=1),
             writes=[("IDF",)])
        S.op("dve", lambda e: e.tensor_reduce(EQ[:], IDF[:].rearrange("p (t q) -> p q t", q=16),
                                              AX.X, ALU.add),
             reads=[("IDF",)], writes=[("EQ",)])
        for t in range(8):
            S.op("dve", lambda e, t=t: e.tensor_copy(DM[:, t, :], EQ[:]), reads=[("EQ",)], writes=[("DM",)])
        S.op("pool", lambda e: e.affine_select(DM[:], DM[:], [[16, 8], [0, 16]], ALU.is_ge, 0.0,
                                               base=15, channel_multiplier=-1),
             reads=[("DM",)], writes=[("DM",)])

        def mmgroup(out_ap, bank, pairs):
            n = len(pairs)
            for i, (l_, r_, rk) in enumerate(pairs):
                S.op("pe", lambda e, l_=l_, r_=r_, i=i: e.matmul(out_ap, l_, r_, start=(i == 0), stop=(i == n - 1)),
                     reads=rk, writes=[("ps", bank)], inc=(1 if i == n - 1 else 0))

        def rstd_from_sum(bank, n, inv_n):
            ri = ring("rb", 3)
            rb = RB[:, ri, 0:n]
            S.op("act", lambda e: e.activation(rb, PS[:, bank, 0:n], AF.Ln, bias=EPSC[:, 0:1], scale=inv_n),
                 reads=[("EPSC",)], writes=[("ps", bank), ("RB", ri)])
            S.op("act", lambda e: e.activation(rb, rb, AF.Exp, scale=-0.5),
                 writes=[("RB", ri)])
            return ri, rb

        def rmsnorm_block(bi, c0, n, src, srckey, chunks, gcol, dst, dstkey, inv_n, extra_reads=(), fence=()):
            bank = newbank()
            nchk = len(chunks)
            for i, c in enumerate(chunks):
                qi = ring("sq", 4)
                sq = SQ[:, qi, 0:n]
                S.op("act", lambda e, c=c, sq=sq: e.activation(sq, src[:, c, c0:c0 + n], AF.Square),
                     reads=[(srckey, c, bi)] + list(extra_reads), writes=[("SQ", qi)])
                S.op("pe", lambda e, sq=sq, i=i: e.matmul(PS[:, bank, 0:n], ONES[:], sq,
                                                          start=(i == 0), stop=(i == nchk - 1)),
                     reads=[("SQ", qi), ("ONES",)], writes=[("ps", bank)], inc=1)
            ri, rb = rstd_from_sum(bank, n, inv_n)
            for i, c in enumerate(chunks):
                eng = "dve"
                S.op(eng, lambda e, c=c: e.scalar_tensor_tensor(dst[:, c, c0:c0 + n], src[:, c, c0:c0 + n],
                                                                 gcol(c), rb, ALU.mult, ALU.mult),
                     reads=[(srckey, c, bi), ("RB", ri), ("VEC",)] + list(extra_reads),
                     writes=[(dstkey, c, bi)], fence=fence)

        def rmsnorm_async(bi, c0, n, src, srckey, chunks, gcol, dst, dstkey, inv_n, extra_reads=(), fence=(), split=False):
            stt = {}
            nchk = len(chunks)

            LAG = 3

            def s_chunk(i):
                def f():
                    if i < nchk:
                        c = chunks[i]
                        qi = ring("sq", 4)
                        sq = SQ[:, qi, 0:n]
                        stt[("sq", i)] = (qi, sq)
                        S.op("act", lambda e: e.activation(sq, src[:, c, c0:c0 + n], AF.Square),
                             reads=[(srckey, c, bi)] + list(extra_reads), writes=[("SQ", qi)])
                    k = i - LAG
                    if k >= 0:
                        if k == 0:
                            stt["bank"] = statbank()
                        bank = stt["bank"]
                        qj, sqj = stt[("sq", k)]
                        S.op("pe", lambda e: e.matmul(PS[:, bank, 0:n], ONES[:], sqj, start=(k == 0), stop=(k == nchk - 1)),
                             reads=[("SQ", qj), ("ONES",)], writes=[("ps", bank)], inc=1)
                return f

            def s_rstd():
                stt["ri"], stt["rb"] = rstd_from_sum(stt["bank"], n, inv_n)

            def s_mul(cs):
                def f():
                    ri, rb = stt["ri"], stt["rb"]
                    for c in cs:
                        S.op("dve", lambda e, c=c: e.scalar_tensor_tensor(dst[:, c, c0:c0 + n], src[:, c, c0:c0 + n],
                                                                           gcol(c), rb, ALU.mult, ALU.mult),
                             reads=[(srckey, c, bi), ("RB", ri), ("VEC",)] + list(extra_reads),
                             writes=[(dstkey, c, bi)], fence=fence)
                return f
            steps = [s_chunk(i) for i in range(nchk + LAG)] + [s_rstd]
            half = (nchk + 1) // 2
            msteps = [s_mul(chunks[:half]), s_mul(chunks[half:])]
            if split:
                bg_add(steps)
                return lambda: bg_add(msteps)
            return bg_add(steps + msteps)

        def transpose_in(src_rows_ap, nrows, ncolchunks, evac):
            pass

        def dbg(*tag):
            if _DBG == tag:
                S.stopped = True

        for hf in range(2):
            if hf == 0:
                blocks = [(0, 512, "P"), (512, 512, "P"), (1024, 128, "S")]
                tiles = [(i * 128, i * 128, "P") for i in range(8)] + [(1024, 2048, "S")]
            else:
                blocks = [(0, 384, "P"), (384, 384, "P"), (768, 256, "P")]
                tiles = [(i * 128, 1024 + i * 128, "P") for i in range(8)]
            lastp = max(i for i, b in enumerate(blocks) if b[2] == "P")
            sblk = len(blocks) - 1
            nblk = len(blocks)
            ntile = len(tiles)
            nrm_h = {}

            def blk_of_col(c):
                for bi, (c0, n, _) in enumerate(blocks):
                    if c0 <= c < c0 + n:
                        return bi
                raise AssertionError

            for (col0, row0, _) in tiles:
                si = ring("stg", 2)
                bi = blk_of_col(col0)
                dma(STG[:, si, :], xin[row0:row0 + 128, :], writes=[("STG", si)])
                for half8 in range(2):
                    bank = newbank()
                    for k4 in range(4):
                        kc = half8 * 4 + k4
                        S.op("pe", lambda e, kc=kc, k4=k4, bank=bank, si=si: e.transpose(
                            PS[:, bank, k4 * 128:(k4 + 1) * 128], STG[:, si, kc * 128:(kc + 1) * 128], IDF[:]),
                            reads=[("STG", si), ("IDF",)], writes=[("ps", bank)], inc=(1 if k4 == 3 else 0))
                    S.op("act" if half8 == 0 else "dve",
                         (lambda e, bank=bank, half8=half8, col0=col0: e.activation(
                             H[:, half8 * 4:half8 * 4 + 4, col0:col0 + 128],
                             PS[:, bank, :].rearrange("p (k c) -> p k c", k=4), AF.Copy)) if half8 == 0 else
                         (lambda e, bank=bank, half8=half8, col0=col0: e.tensor_copy(
                             H[:, half8 * 4:half8 * 4 + 4, col0:col0 + 128],
                             PS[:, bank, :].rearrange("p (k c) -> p k c", k=4))),
                         writes=[("ps", bank)] + [("H", half8 * 4 + k, bi) for k in range(4)])

            for l in range(DEPTH):
                dma(LNB[:], lnb[l], writes=[("LNB",)])
                WST = VG[:, 0:2, :].rearrange("p a (b t) -> p (a b) t", t=128)
                dma(WST, wsT[l], writes=[("VG", 0), ("VG", 1)])
                dma(WSA[:], wsA[l], writes=[("WSA",)])
                dma(BSP[:], bsP[l], writes=[("BSP",)])
                dma(BSS[:], bsS[l], writes=[("BSS",)])
                S.op("pool", lambda e: e.affine_select(MP[:], WST, [[0, 6], [1, 128]], ALU.is_ge, 0.0,
                                                       base=0, channel_multiplier=-1),
                     reads=[("VG", 0), ("VG", 1)], writes=[("MP",)])
                if hf == 0:
                    for h in range(6):
                        S.op("pool", lambda e, h=h: e.tensor_tensor(
                            MS[:, h, :].rearrange("p (t q) -> p t q", q=16),
                            WSA[:, h, :].unsqueeze(2).to_broadcast([128, 8, 16]), DM[:], ALU.mult),
                            reads=[("WSA",), ("DM",)], writes=[("MS",)])
                hist_links = []

                def pop_link(all_=False):
                    while hist_links:
                        hist_links.pop(0)()
                        if not all_:
                            break

                if hf == 0:
                    for j in range(2):
                        S.op("pool", lambda e, j=j: e.memset(GB[:, j, 0:30], 0.0), writes=[("Gh", j)])
                    def link_sa(l=l):
                        si = ring("stg", 2)
                        dma(STG[0:32, si, 0:384], sa[l], writes=[("STG", si)])
                        bank = newbank()
                        for j in range(3):
                            S.op("pe", lambda e, j=j, bank=bank, si=si: e.transpose(
                                PS[:, bank, j * 32:(j + 1) * 32], STG[0:32, si, j * 128:(j + 1) * 128], IDF[0:32, 0:32]),
                                reads=[("STG", si), ("IDF",)], writes=[("ps", bank)], inc=(1 if j == 2 else 0))
                        S.op("dve", lambda e, bank=bank: e.tensor_copy(
                            XAH[:, :, :], PS[:, bank, 0:96].rearrange("p (j c) -> p j c", j=3)),
                            writes=[("ps", bank), ("XAH",)])

                    def link_sc(rt, l=l):
                        def f():
                            nr = 128 if rt < 3 else 96
                            si = ring("stg", 2)
                            dma(STG[0:nr, si, 0:256], sc[l, rt * 128:rt * 128 + nr, :], writes=[("STG", si)])
                            bank = newbank()
                            for j in range(2):
                                S.op("pe", lambda e, j=j, bank=bank, si=si, nr=nr: e.transpose(
                                    PS[:, bank, j * 128:j * 128 + nr], STG[0:nr, si, j * 128:(j + 1) * 128],
                                    IDF[0:nr, 0:nr]),
                                    reads=[("STG", si), ("IDF",)], writes=[("ps", bank)], inc=(1 if j == 1 else 0))
                            S.op("dve", lambda e, bank=bank, rt=rt, nr=nr: e.tensor_copy(
                                GFS[:, :, rt * 128:rt * 128 + nr],
                                PS[:, bank, 0:256].rearrange("p (j c) -> p j c", j=2)[:, :, 0:nr]),
                                writes=[("ps", bank)] + [("GFS", j) for j in range(2)])
                            if rt == 3:
                                for j in range(2):
                                    S.op("pool", lambda e, j=j: e.tensor_copy(GB[:, j, GS0:GS0 + 480], GFS[:, j, 0:480]),
                                         reads=[("GFS", j)], writes=[("Gsh", j)])
                        return f
                    hist_links.extend([link_sc(0), link_sc(1), link_sc(2), link_sc(3), link_sa])
                else:
                    for j in range(2):
                        S.op("pool", lambda e, l=l, j=j: e.tensor_copy(GB[:, j, 0:30], CARG[:, l, j, :]),
                             reads=[("CARG", l)], writes=[("Gh", j)])

                pt_h = {}
                pload, ptrans = [], []
                for ti_, (col0, row0, _) in enumerate(tiles):
                    pi = ti_ % 4

                    def pld(pi=pi, row0=row0, l=l):
                        dma(PSTG[:, pi, :], pin[l, row0:row0 + 128, :], writes=[("PSTG", pi)])
                    pload.append(pld)

                    def pstep(pi=pi, col0=col0):
                        bank = newbank()
                        for j in range(2):
                            S.op("pe", lambda e, j=j, bank=bank, pi=pi: e.transpose(
                                PS[:, bank, j * 128:(j + 1) * 128], PSTG[:, pi, j * 128:(j + 1) * 128], IDF[:]),
                                reads=[("PSTG", pi), ("IDF",)], writes=[("ps", bank)], inc=(1 if j == 1 else 0))
                        S.op("act", lambda e, bank=bank, col0=col0: e.activation(
                            PT[:, :, col0:col0 + 128], PS[:, bank, 0:256].rearrange("p (j c) -> p j c", j=2), AF.Copy),
                            writes=[("ps", bank), ("PT", col0 // 128)])
                    ptrans.append(pstep)
                for ti_ in range(min(4, ntile)):
                    bg_add([pload[ti_]])
                for ti_ in range(ntile):
                    pt_h[ti_] = bg_add([ptrans[ti_]])
                    if ti_ + 4 < ntile:
                        bg_add([pload[ti_ + 4]])

                if l == 0:
                    for bi, (c0, n, _) in enumerate(blocks):
                        nrm_h[bi] = rmsnorm_async(bi, c0, n, H, "H", list(range(8)), lambda c, l=l: vcol(l, 0, c),
                                                  NRM, "NRM", 1.0 / D)

                S.op("dve", lambda e, b0_=l * NVL + 41: e.tensor_tensor(
                    DG[:], IDF[:].unsqueeze(1).to_broadcast([128, 31, 128]),
                    VEC[:, b0_:b0_ + 62:2].unsqueeze(2).to_broadcast([128, 31, 128]), ALU.mult),
                    reads=[("IDF",), ("VEC",)], writes=[("DG",)])
                siu, wvu = wacq("U")
                si, wv = wacq("V")

                def u_block(bi):
                    c0, n, kind = blocks[bi]
                    ensure(nrm_h[bi])
                    for j in range(3):
                        bk = newbank()
                        mmgroup(PS[:, bk, 0:n], bk,
                                [(wvu[:, kc, j * 128:(j + 1) * 128], NRM[:, kc, c0:c0 + n],
                                  [("slot", siu), ("NRM", kc, bi)]) for kc in range(8)])
                        S.op("act", lambda e, bk=bk, j=j, c0=c0, n=n: e.activation(
                            Y[:, 3 + j, c0:c0 + n], PS[:, bk, 0:n], AF.Gelu),
                            reads=[("Yreg",)], writes=[("ps", bk), ("Y", 3 + j, bi)], fence=[("Areg",)])
                        pump(st["pumpk"])

                u_block(0)
                pop_link()
                for tbi, tb in enumerate(range(0, ntile, 3)):
                    batch = list(range(tb, min(tb + 3, ntile)))
                    pop_link()
                    for ti_ in batch:
                        col0, row0, kind = tiles[ti_]
                        bi = blk_of_col(col0)
                        ensure(nrm_h[bi])
                        bk = newbank()
                        mmgroup(PS[:, bk, 0:384], bk,
                                [(NRM[:, kc, col0:col0 + 128], wv[:, kc, :],
                                  [("slot", si), ("NRM", kc, bi)]) for kc in range(8)])
                        gi = ti_ % 3
                        S.op("act", lambda e, bk=bk, gi=gi: e.activation(VG[:, gi, :], PS[:, bk, 0:384], AF.Gelu),
                             writes=[("ps", bk), ("VG", gi)])
                        S.op("dve", lambda e, gi=gi: e.bn_stats(BST[:, gi, 0:6], VG[:, gi, :]),
                             reads=[("VG", gi)], writes=[("BST", gi)])
                        S.op("dve", lambda e, gi=gi: e.bn_aggr(BST[:, gi, 6:8], BST[:, gi, 0:6]),
                             writes=[("BST", gi)])
                        pump(st["pumpk"])
                    if tbi + 1 < nblk:
                        u_block(tbi + 1)
                    for ti_ in batch:
                        col0, row0, kind = tiles[ti_]
                        gi = ti_ % 3
                        S.op("act", lambda e, gi=gi: e.activation(BST[:, gi, 7:8], BST[:, gi, 7:8], AF.Ln, bias=EPSC[:, 0:1]),
                             reads=[("EPSC",)], writes=[("BST", gi)])
                        S.op("act", lambda e, gi=gi: e.activation(BST[:, gi, 7:8], BST[:, gi, 7:8], AF.Exp, scale=-0.5),
                             writes=[("BST", gi)])
                        S.op("dve", lambda e, gi=gi: e.tensor_scalar(VG[:, gi, :], VG[:, gi, :], BST[:, gi, 6:7],
                                                                     BST[:, gi, 7:8], ALU.subtract, ALU.mult),
                             reads=[("BST", gi)], writes=[("VG", gi)])
                        S.op("pool", lambda e, gi=gi: e.tensor_tensor(VG[:, gi, :], VG[:, gi, :], LNB[:, 0, :], ALU.mult),
                             reads=[("LNB",)], writes=[("VG", gi)])
                        is_out = (hf == 0 and kind == "S") or (hf == 1 and ti_ == ntile - 1)
                        if is_out:
                            S.op("pool", lambda e, gi=gi: e.tensor_tensor(VNF[:], VG[:, gi, :], LNB[:, 1, :], ALU.add),
                                 reads=[("LNB",), ("VG", gi)], writes=[("VNF",)])
                            S.op("pool", lambda e, ti_=ti_: e.tensor_copy(VN[:, ti_, :], VNF[:]),
                                 reads=[("VNF",)], writes=[("VN", ti_)])
                            dma((nvs_o if kind == "S" else nvp_o)[l], VNF[:], reads=[("VNF",)])
                        else:
                            S.op("pool", lambda e, gi=gi, ti_=ti_: e.tensor_tensor(VN[:, ti_, :], VG[:, gi, :], LNB[:, 1, :], ALU.add),
                                 reads=[("LNB",), ("VG", gi)], writes=[("VN", ti_)])
                assert (ntile + 2) // 3 >= nblk - 1
                wrel(2)
                pop_link(all_=True)
                siv, wvv = wacq("CV")
                sig, wvg = wacq("CG")
                for bi, (c0, n, kind) in enumerate(blocks):
                    for j in range(2):
                        bv, bg_ = newbank(), newbank()
                        mmgroup(PS[:, bv, 0:n], bv,
                                [(wvv[:, kc, j * 128:(j + 1) * 128], NRM[:, kc, c0:c0 + n],
                                  [("slot", siv), ("NRM", kc, bi)]) for kc in range(8)])
                        mmgroup(PS[:, bg_, 0:n], bg_,
                                [(wvg[:, kc, j * 128:(j + 1) * 128], NRM[:, kc, c0:c0 + n],
                                  [("slot", sig), ("NRM", kc, bi)]) for kc in range(8)])
                        ti = ring("tmp", 3)
                        tmp = TMP[:, ti, 0:n]
                        S.op("act", lambda e, bg_=bg_, tmp=tmp, n=n: e.activation(tmp, PS[:, bg_, 0:n], AF.Sigmoid),
                             writes=[("ps", bg_), ("TMP", ti)])
                        gc = (30 + c0) if kind == "P" else (GS0 + 480)
                        S.op("dve", lambda e, bv=bv, tmp=tmp, n=n, gc=gc, j=j: e.tensor_tensor(
                            GB[:, j, gc:gc + n], tmp, PS[:, bv, 0:n], ALU.mult),
                            reads=[("TMP", ti)], writes=[("ps", bv), ("G", j, bi)])
                        if kind == "P" and c0 + n == 1024:
                            S.op("dve", lambda e, bv=bv, tmp=tmp, n=n, j=j: e.tensor_tensor(
                                GFT[:, j, :], tmp[:, n - 30:n], PS[:, bv, n - 30:n], ALU.mult),
                                reads=[("TMP", ti)], writes=[("ps", bv), ("GFT", j)])
                        if kind == "S":
                            S.op("dve", lambda e, bv=bv, tmp=tmp, n=n, j=j: e.tensor_tensor(
                                GFS[:, j, 480:608], tmp, PS[:, bv, 0:n], ALU.mult),
                                reads=[("TMP", ti)], writes=[("ps", bv), ("GFS", j)])
                        pump(st["pumpk"])
                wrel(2)
                if hf == 0:
                    for j in range(2):
                        S.op("pool", lambda e, l=l, j=j: e.tensor_copy(CARG[:, l, j, :], GFT[:, j, :]),
                             reads=[("GFT", j)], writes=[("CARG", l)])
                    for rt in range(4):
                        nr = 128 if rt < 3 else 96
                        bank = newbank()
                        for j in range(2):
                            S.op("pe", lambda e, j=j, bank=bank, rt=rt, nr=nr: e.transpose(
                                PS[0:nr, bank, j * 128:(j + 1) * 128],
                                GFS[:, j, 128 + rt * 128:128 + rt * 128 + nr], IDF[:]),
                                reads=[("GFS", j), ("IDF",)], writes=[("ps", bank)], inc=(1 if j == 1 else 0))
                        si_ = ring("stg", 2)
                        S.op("dve", lambda e, bank=bank, si_=si_, nr=nr: e.tensor_copy(STG[0:nr, si_, 0:256], PS[0:nr, bank, 0:256]),
                             writes=[("ps", bank), ("STG", si_)])
                        dma(ncs_o[l, rt * 128:rt * 128 + nr, :], STG[0:nr, si_, 0:256], reads=[("STG", si_)])
                else:
                    bank = newbank()
                    for j in range(2):
                        S.op("pe", lambda e, j=j, bank=bank: e.transpose(
                            PS[0:30, bank, j * 128:(j + 1) * 128], GFT[:, j, :], IDF[:]),
                            reads=[("GFT", j), ("IDF",)], writes=[("ps", bank)], inc=(1 if j == 1 else 0))
                    si_ = ring("stg", 2)
                    S.op("dve", lambda e, bank=bank, si_=si_: e.tensor_copy(STG[0:30, si_, 0:256], PS[0:30, bank, 0:256]),
                         writes=[("ps", bank), ("STG", si_)])
                    dma(ncp_o[l], STG[0:30, si_, 0:256], reads=[("STG", si_)])

                def build_dg(j):
                    b0_ = l * NVL + 41 + j
                    S.op("dve", lambda e, b0_=b0_: e.tensor_tensor(
                        DG[:], IDF[:].unsqueeze(1).to_broadcast([128, 31, 128]),
                        VEC[:, b0_:b0_ + 62:2].unsqueeze(2).to_broadcast([128, 31, 128]), ALU.mult),
                        reads=[("IDF",), ("VEC",)], writes=[("DG",)])

                def conv31(j):
                    for bi, (c0, n, kind) in enumerate(blocks):
                        if kind == "P":
                            taps = [c0 + k for k in range(31)]
                            hk = [("G", j, bi - 1)] if bi > 0 else [("Gh", j)]
                        else:
                            taps = [GS0 + 16 * k for k in range(31)]
                            hk = [("Gsh", j)]
                        bk = newbank()
                        mmgroup(PS[:, bk, 0:n], bk,
                                [(DG[:, k, :], GB[:, j, taps[k]:taps[k] + n], [("DG",), ("G", j, bi)] + hk)
                                 for k in range(31)])
                        S.op("act", lambda e, l=l, j=j, bk=bk, c0=c0, n=n: e.activation(
                            Y[:, 6 + j, c0:c0 + n], PS[:, bk, 0:n], AF.Identity, bias=vcol(l, 103, j)),
                            reads=[("VEC",), ("Yreg",)], writes=[("ps", bk), ("Y", 6 + j, bi)], fence=[("Areg",)])
                        pump(st["pumpk"])

                def lnc_async(bi, c0, n):
                    stt = {}

                    def s_act(js):
                        def f():
                            for i in js:
                                func = AF.Copy if i < 2 else AF.Square
                                j = i % 2
                                qi = ring("sq", 4)
                                sq = SQ[:, qi, 0:n]
                                stt[("sq", i)] = (qi, sq)
                                S.op("act", lambda e, sq=sq, j=j, func=func: e.activation(sq, Y[:, 6 + j, c0:c0 + n], func),
                                     reads=[("Y", 6 + j, bi)], writes=[("SQ", qi)])
                        return f

                    def s_mm(ks):
                        def f():
                            for k in ks:
                                if k == 0:
                                    stt["bm"], stt["bq"] = statbank(), statbank()
                                bk_ = stt["bm"] if k < 2 else stt["bq"]
                                qj, sqj = stt[("sq", k)]
                                S.op("pe", lambda e, bk_=bk_, sqj=sqj, k=k: e.matmul(PS[:, bk_, 0:n], ONES[:], sqj, start=(k % 2 == 0), stop=(k % 2 == 1)),
                                     reads=[("SQ", qj), ("ONES",)], writes=[("ps", bk_)], inc=1)
                        return f

                    def s_fin():
                        bm, bq = stt["bm"], stt["bq"]
                        mi = ring("rb", 3)
                        mb = RB[:, mi, 0:n]
                        S.op("act", lambda e: e.activation(mb, PS[:, bm, 0:n], AF.Copy, scale=1.0 / 256),
                             writes=[("ps", bm), ("RB", mi)])
                        vi = ring("rb", 3)
                        vb = RB[:, vi, 0:n]
                        S.op("dve", lambda e: e.tensor_tensor(vb, mb, mb, ALU.mult),
                             reads=[("RB", mi)], writes=[("RB", vi)])
                        S.op("dve", lambda e: e.scalar_tensor_tensor(vb, PS[:, bq, 0:n], 1.0 / 256, vb, ALU.mult, ALU.subtract),
                             writes=[("ps", bq), ("RB", vi)])
                        S.op("act", lambda e: e.activation(vb, vb, AF.Ln, bias=EPSC[:, 0:1]),
                             reads=[("EPSC",)], writes=[("RB", vi)])
                        S.op("act", lambda e: e.activation(vb, vb, AF.Exp, scale=-0.5), writes=[("RB", vi)])
                        for j in range(2):
                            yc = Y[:, 6 + j, c0:c0 + n]
                            S.op("dve", lambda e, yc=yc: e.tensor_tensor(yc, yc, mb, ALU.subtract),
                                 reads=[("RB", mi)], writes=[("Y", 6 + j, bi)])
                            S.op("dve", lambda e, yc=yc: e.tensor_tensor(yc, yc, vb, ALU.mult),
                                 reads=[("RB", vi)], writes=[("Y", 6 + j, bi)])
                            S.op("act", lambda e, yc=yc, j=j, l=l: e.activation(yc, yc, AF.Silu, bias=vcol(l, 107, j), scale=vcol(l, 105, j)),
                                 reads=[("VEC",)], writes=[("Y", 6 + j, bi)])
                    return [s_act([0, 1]), s_act([2, 3])], [s_mm([0, 1]), s_mm([2, 3])], s_fin

                def groupA(j, hook_post=None, pre_block=None):
                    xj = j % 2
                    if hf == 0:
                        S.op("pool", lambda e, xj=xj: e.memset(XA[:, xj, 0:2], 0.0), writes=[("XAh", xj)])
                        S.op("pool", lambda e, xj=xj, j=j: e.tensor_copy(XA[:, xj, XS0:XS0 + 32], XAH[:, j, :]),
                             reads=[("XAH",)], writes=[("XAsh", xj)])
                    else:
                        S.op("pool", lambda e, l=l, xj=xj, j=j: e.tensor_copy(XA[:, xj, 0:2], CARA[:, l, j, :]),
                             reads=[("CARA", l, j)], writes=[("XAh", xj)])
                    si, wv = wacq("A")
                    for bi, (c0, n, kind) in enumerate(blocks):
                        if pre_block is not None:
                            pre_block(bi)
                        bx, bc, bb = newbank(), newbank(), newbank()
                        for (bk, off) in ((bx, 0), (bc, 128), (bb, 256)):
                            mmgroup(PS[:, bk, 0:n], bk,
                                    [(wv[:, kc, off:off + 128], NRM[:, kc, c0:c0 + n],
                                      [("slot", si), ("NRM", kc, bi)]) for kc in range(8)])
                        ti = ring("tmp", 3)
                        tmp = TMP[:, ti, 0:n]
                        S.op("act", lambda e, bx=bx, tmp=tmp, n=n: e.activation(tmp, PS[:, bx, 0:n], AF.Copy),
                             writes=[("ps", bx), ("TMP", ti)])
                        xc = (2 + c0) if kind == "P" else (XS0 + 32)
                        S.op("dve", lambda e, bc=bc, tmp=tmp, n=n, xc=xc, xj=xj: e.tensor_tensor(
                            XA[:, xj, xc:xc + n], tmp, PS[:, bc, 0:n], ALU.mult),
                            reads=[("TMP", ti)], writes=[("ps", bc), ("XA", xj, bi)])
                        if kind == "P":
                            taps = [c0 + k for k in range(3)]
                            hk = [("XA", xj, bi - 1)] if bi > 0 else [("XAh", xj)]
                        else:
                            taps = [XS0 + 16 * k for k in range(3)]
                            hk = [("XAsh", xj)]
                        t2 = ring("tmp", 3)
                        acc = TMP[:, t2, 0:n]
                        S.op("dve", lambda e, l=l, acc=acc, j=j, xj=xj, n=n, a=taps[2]: e.tensor_scalar(
                            acc, XA[:, xj, a:a + n], vcol(l, 32, 2 * 3 + j), None, ALU.mult),
                            reads=[("XA", xj, bi), ("VEC",)] + hk, writes=[("TMP", t2)])
                        for k in (1, 0):
                            S.op("dve", lambda e, l=l, acc=acc, j=j, xj=xj, n=n, a=taps[k], k=k: e.scalar_tensor_tensor(
                                acc, XA[:, xj, a:a + n], vcol(l, 32, k * 3 + j), acc, ALU.mult, ALU.add),
                                reads=[("XA", xj, bi), ("VEC",)] + hk, writes=[("TMP", t2)])
                        S.op("dve", lambda e, acc=acc, j=j, n=n, c0=c0, bb=bb: e.tensor_tensor(
                            Y[:, j, c0:c0 + n], acc, PS[:, bb, 0:n], ALU.mult),
                            reads=[("TMP", t2), ("Yreg",)], writes=[("ps", bb), ("Y", j, bi)], fence=[("Areg",)])
                        if hook_post is not None:
                            hook_post(bi)
                        pump(st["pumpk"])
                    wrel()
                    bank = newbank()
                    si_ = ring("stg", 2)
                    if hf == 0:
                        S.op("pool", lambda e, l=l, j=j, xj=xj: e.tensor_copy(CARA[:, l, j, :], XA[:, xj, 1024:1026]),
                             reads=[("XA", xj, lastp)], writes=[("CARA", l, j)])
                        S.op("pe", lambda e, xj=xj, bank=bank: e.transpose(
                            PS[0:32, bank, 0:128], XA[:, xj, XS0 + 128:XS0 + 160], IDF[:]),
                            reads=[("XA", xj, sblk), ("IDF",)], writes=[("ps", bank)])
                        S.op("dve", lambda e, bank=bank, si_=si_: e.tensor_copy(STG[0:32, si_, 0:128], PS[0:32, bank, 0:128]),
                             writes=[("ps", bank), ("STG", si_)])
                        dma(nas_o[l, :, j * 128:(j + 1) * 128], STG[0:32, si_, 0:128], reads=[("STG", si_)])
                    else:
                        S.op("pe", lambda e, xj=xj, bank=bank: e.transpose(
                            PS[0:2, bank, 0:128], XA[:, xj, 1024:1026], IDF[:]),
                            reads=[("XA", xj, lastp), ("IDF",)], writes=[("ps", bank)])
                        S.op("dve", lambda e, bank=bank, si_=si_: e.tensor_copy(STG[0:2, si_, 0:128], PS[0:2, bank, 0:128]),
                             writes=[("ps", bank), ("STG", si_)])
                        dma(nap_o[l, :, j * 128:(j + 1) * 128], STG[0:2, si_, 0:128], reads=[("STG", si_)])

                conv31(0)
                build_dg(1)
                groupA(0)
                conv31(1)
                lparts = [lnc_async(bi, c0, n) for bi, (c0, n, kind) in enumerate(blocks)]
                bg_add(lparts[0][0])
                for bi in range(nblk):
                    bg_add(lparts[bi][1])
                    if bi + 1 < nblk:
                        bg_add(lparts[bi + 1][0])
                    bg_add([lparts[bi][2]])
                groupA(1)
                def mix_block(bi):
                    c0, n, kind = blocks[bi]
                    for hp in range(3):
                        bk = newbank()
                        ntc = n // 128
                        for q4 in range(ntc):
                            ti_ = (c0 // 128) + q4
                            for hh in range(2):
                                h = 2 * hp + hh
                                M_ = MS if kind == "S" else MP
                                S.op("pe", lambda e, bk=bk, q4=q4, hh=hh, h=h, ti_=ti_, M_=M_: e.matmul(
                                    PS[hh * 64:(hh + 1) * 64, bk, q4 * 128:(q4 + 1) * 128],
                                    VN[:, ti_, h * 64:(h + 1) * 64], M_[:, h, :], start=True, stop=True),
                                    reads=[("VN", ti_), ("MS",) if kind == "S" else ("MP",)], writes=[("ps", bk)],
                                    inc=(1 if (q4 == ntc - 1 and hh == 1) else 0))
                        t2 = ring("tmp", 3)
                        tmp = TMP[:, t2, 0:n]
                        BS_ = BSS if kind == "S" else BSP
                        S.op("dve", lambda e, bk=bk, tmp=tmp, n=n, ntc=ntc, BS_=BS_, hp=hp: e.tensor_tensor(
                            tmp.rearrange("p (a t) -> p a t", a=ntc), PS[:, bk, 0:n].rearrange("p (a t) -> p a t", a=ntc),
                            BS_[:, hp, :].unsqueeze(1).to_broadcast([128, ntc, 128]), ALU.add),
                            reads=[("BSS",) if kind == "S" else ("BSP",)], writes=[("ps", bk), ("TMP", t2)])
                        S.op("dve", lambda e, tmp=tmp, c0=c0, n=n, hp=hp: e.tensor_tensor(
                            Y[:, 3 + hp, c0:c0 + n], tmp, Y[:, 3 + hp, c0:c0 + n], ALU.mult),
                            reads=[("TMP", t2), ("Yreg",)], writes=[("Y", 3 + hp, bi)])
                        pump(3)
                gn_h = {}
                gn0 = []

                def pre_block(bi):
                    mix_block(bi)
                    if bi == 0:
                        c0_, n_, _k = blocks[0]
                        gn0.extend(rmsnorm_async(0, c0_, n_, Y, "Y", list(chunks), lambda c, l=l: vcol(l, 24, c),
                                                 NRM, "NRM", inv_n, extra_reads=[("Yreg",)], split=True)
                                   for (chunks, inv_n) in (((3, 4, 5), 1.0 / 384), ((6, 7), 1.0 / 256)))

                def gn_hook(bi):
                    c0, n, kind = blocks[bi]
                    if bi == 0:
                        hs = [mk() for mk in gn0]
                        hs.append(rmsnorm_async(bi, c0, n, Y, "Y", [0, 1, 2], lambda c, l=l: vcol(l, 24, c),
                                                NRM, "NRM", 1.0 / 384, extra_reads=[("Yreg",)]))
                        gn_h[bi] = hs
                        return
                    gn_h[bi] = [rmsnorm_async(bi, c0, n, Y, "Y", list(chunks), lambda c, l=l: vcol(l, 24, c),
                                              NRM, "NRM", inv_n, extra_reads=[("Yreg",)])
                                for (chunks, inv_n) in (((3, 4, 5), 1.0 / 384), ((6, 7), 1.0 / 256), ((0, 1, 2), 1.0 / 384))]
                st["pumpk"] = 4
                groupA(2, hook_post=gn_hook, pre_block=pre_block)
                dbg(hf, l, 'B')

                osl = [wacq("O") for _ in range(3)]
                for bi, (c0, n, kind) in enumerate(blocks):
                    for h_ in gn_h[bi]:
                        ensure(h_)
                    for m in range(8):
                        si, wv = osl[m // 3]
                        mo = (m % 3) * 128
                        bk = newbank()
                        mmgroup(PS[:, bk, 0:n], bk,
                                [(wv[:, kc, mo:mo + 128], NRM[:, kc, c0:c0 + n],
                                  [("slot", si), ("NRM", kc, bi)]) for kc in range(8)])
                        S.op("dve", lambda e, bk=bk, m=m, c0=c0, n=n: e.tensor_tensor(
                            H[:, m, c0:c0 + n], H[:, m, c0:c0 + n], PS[:, bk, 0:n], ALU.add),
                            writes=[("ps", bk), ("H", m, bi)])
                        if bi == nblk - 1 and m in (2, 5):
                            wrel(1)
                        pump(st["pumpk"])
                    nrm_h[bi] = rmsnorm_async(bi, c0, n, H, "H", list(range(8)), lambda c, l=l: vcol(l, 8, c),
                                              NRM, "NRM", 1.0 / D)
                st["pumpk"] = 3
                wrel(1)
                dbg(hf, l, 'O')
                for pc in range(3):
                    nj = (8, 7, 7)[pc]
                    for (a, w_) in ((0, 3), (3, 3), (6, nj - 6)):
                        sg, wg_ = wacq("FG")
                        su, wu_ = wacq("FU")
                        for bi, (c0, n, kind) in enumerate(blocks):
                            ensure(nrm_h[bi])
                            for jj in range(w_):
                                jl = a + jj
                                bg_, bu = newbank(), newbank()
                                mmgroup(PS[:, bg_, 0:n], bg_,
                                        [(wg_[:, kc, jj * 128:(jj + 1) * 128], NRM[:, kc, c0:c0 + n],
                                          [("slot", sg), ("NRM", kc, bi)]) for kc in range(8)])
                                mmgroup(PS[:, bu, 0:n], bu,
                                        [(wu_[:, kc, jj * 128:(jj + 1) * 128], NRM[:, kc, c0:c0 + n],
                                          [("slot", su), ("NRM", kc, bi)]) for kc in range(8)])
                                ti = ring("tmp", 3)
                                tmp = TMP[:, ti, 0:n]
                                S.op("act", lambda e, bg_=bg_, tmp=tmp, n=n: e.activation(tmp, PS[:, bg_, 0:n], AF.Silu),
                                     writes=[("ps", bg_), ("TMP", ti)])
                                S.op("dve", lambda e, bu=bu, tmp=tmp, jl=jl, c0=c0, n=n: e.tensor_tensor(
                                    ACTB[:, jl, c0:c0 + n], tmp, PS[:, bu, 0:n], ALU.mult),
                                    reads=[("TMP", ti), ("Areg",)], writes=[("ps", bu), ("ACTB", jl, bi)], fence=[("Yreg",)])
                                pump(st["pumpk"])
                        wrel(2)
                    dsl = [wacq("FD"), wacq("FD"), wacq("FD")]
                    last = (pc == 2)
                    for bi, (c0, n, kind) in enumerate(blocks):
                        for m in range(8):
                            bk = newbank()
                            pairs = []
                            for jl in range(nj):
                                si, wv = dsl[jl // 3]
                                pairs.append((wv[:, jl % 3, m * 128:(m + 1) * 128], ACTB[:, jl, c0:c0 + n],
                                              [("slot", si), ("ACTB", jl, bi), ("Areg",)]))
                            mmgroup(PS[:, bk, 0:n], bk, pairs)
                            S.op("dve", lambda e, bk=bk, m=m, c0=c0, n=n: e.tensor_tensor(
                                H[:, m, c0:c0 + n], H[:, m, c0:c0 + n], PS[:, bk, 0:n], ALU.add),
                                writes=[("ps", bk), ("H", m, bi)])
                            pump(st["pumpk"])
                        if last:
                            nrm_h[bi] = rmsnorm_async(bi, c0, n, H, "H", list(range(8)), lambda c, l=l: vcol(l, 16, c),
                                                      NRM, "NRM", 1.0 / D)
                    wrel(3)
                dbg(hf, l, 'F')
                psl = [wacq("PG")]
                spp, wpp_ = wacq("PP")
                psl += [wacq("PG"), wacq("PG")]
                for bi, (c0, n, kind) in enumerate(blocks):
                    ensure(nrm_h[bi])
                    for t_ in range(c0 // 128, (c0 + n) // 128):
                        ensure(pt_h[t_])
                    for m in range(8):
                        si, wv = psl[m // 3]
                        mo = (m % 3) * 128
                        bg_, bp = newbank(), newbank()
                        mmgroup(PS[:, bg_, 0:n], bg_,
                                [(wv[:, kc, mo:mo + 128], NRM[:, kc, c0:c0 + n],
                                  [("slot", si), ("NRM", kc, bi)]) for kc in range(8)])
                        mmgroup(PS[:, bp, 0:n], bp,
                                [(wpp_[:, kc, m * 128:(m + 1) * 128], PT[:, kc, c0:c0 + n],
                                  [("slot", spp)] + [("PT", t_) for t_ in range(c0 // 128, (c0 + n) // 128)])
                                 for kc in range(2)])
                        ti = ring("tmp", 3)
                        tmp = TMP[:, ti, 0:n]
                        S.op("act", lambda e, bg_=bg_, tmp=tmp, n=n: e.activation(tmp, PS[:, bg_, 0:n], AF.Sigmoid),
                             writes=[("ps", bg_), ("TMP", ti)])
                        S.op("dve", lambda e, bp=bp, tmp=tmp, n=n: e.tensor_tensor(tmp, tmp, PS[:, bp, 0:n], ALU.mult),
                             writes=[("ps", bp), ("TMP", ti)])
                        S.op("dve", lambda e, tmp=tmp, m=m, c0=c0, n=n: e.tensor_tensor(
                            H[:, m, c0:c0 + n], H[:, m, c0:c0 + n], tmp, ALU.add),
                            reads=[("TMP", ti)], writes=[("H", m, bi)])
                        if bi == nblk - 1 and m == 2:
                            wrel(1)
                        pump(st["pumpk"])
                    if l < DEPTH - 1:
                        nrm_h[bi] = rmsnorm_async(bi, c0, n, H, "H", list(range(8)), lambda c, l=l: vcol(l + 1, 0, c),
                                                  NRM, "NRM", 1.0 / D)
                    else:
                        nrm_h[bi] = rmsnorm_async(bi, c0, n, H, "H", list(range(8)),
                                                  lambda c: VEC[:, DEPTH * NVL + c:DEPTH * NVL + c + 1],
                                                  Y, "Y", 1.0 / D, extra_reads=[("Yreg",)], fence=[("Areg",)])
                wrel(3)
                dbg(hf, l, 'P')

            flush()
            for (col0, row0, _) in tiles:
                bi = blk_of_col(col0)
                si_ = ring("stg", 2)
                for half8 in range(2):
                    bank = newbank()
                    for k4 in range(4):
                        kc = half8 * 4 + k4
                        S.op("pe", lambda e, kc=kc, k4=k4, bank=bank, col0=col0: e.transpose(
                            PS[:, bank, k4 * 128:(k4 + 1) * 128], Y[:, kc, col0:col0 + 128], IDF[:]),
                            reads=[("Y", kc, bi), ("IDF",), ("Yreg",)], writes=[("ps", bank)], inc=(1 if k4 == 3 else 0))
                    if half8 == 0:
                        S.op("act", lambda e, bank=bank, si_=si_: e.activation(STG[:, si_, 0:512], PS[:, bank, :], AF.Copy),
                             writes=[("ps", bank), ("STG", si_)])
                    else:
                        S.op("dve", lambda e, bank=bank, si_=si_: e.tensor_copy(STG[:, si_, 512:1024], PS[:, bank, :]),
                             writes=[("ps", bank), ("STG", si_)])
                dma(y_o[row0:row0 + 128, :], STG[:, si_, :], reads=[("STG", si_)])

        assert _DBG is not None or wst["next_acq"] == len(plan), (wst["next_acq"], len(plan))
        finals = [(f"d{i}", S.cnt.get(f"d{i}", 0)) for i in range(NDS)]

        block = es.enter_context(nc.Block())

        def emit(e, name):
            for (wl, fn, semname, inc) in S.ops[name]:
                for (s, v) in wl:
                    e.wait_ge(SEM[s], v)
                ins = fn(e)
                if inc:
                    ins.then_inc(SEM[semname], inc)

        @block.tensor
        def _(e):
            emit(e, "pe")

        @block.scalar
        def _(e):
            emit(e, "act")

        @block.vector
        def _(e):
            emit(e, "dve")

        @block.gpsimd
        def _(e):
            emit(e, "pool")

        @block.sync
        def _(e):
            emit(e, "sp")
            for (s, v) in finals:
                if v:
                    e.wait_ge(SEM[s], v)
    return nc


_NC_CACHE = {}


def _host_layout(inp):
    f = np.float32
    vec = np.zeros((128, NV), f)
    for l in range(DEPTH):
        o = l * NVL
        vec[:, o + 0:o + 8] = inp["g_mix"][l].reshape(8, 128).T
        vec[:, o + 8:o + 16] = inp["g_ffn"][l].reshape(8, 128).T
        vec[:, o + 16:o + 24] = inp["g_ple"][l].reshape(8, 128).T
        vec[:, o + 24:o + 27] = inp["g_out_a"][l].reshape(3, 128).T
        vec[:, o + 27:o + 30] = inp["g_out_b"][l].reshape(3, 128).T
        vec[:, o + 30:o + 32] = inp["g_out_c"][l].reshape(2, 128).T
        vec[:, o + 32:o + 41] = inp["conv_a_w"][l].reshape(3, 3, 128).transpose(2, 0, 1).reshape(128, 9)
        vec[:, o + 41:o + 103] = inp["conv_c_w"][l].reshape(31, 2, 128).transpose(2, 0, 1).reshape(128, 62)
        vec[:, o + 103:o + 105] = inp["conv_c_b"][l].reshape(2, 128).T
        vec[:, o + 105:o + 107] = inp["ln_c_g"][l].reshape(2, 128).T
        vec[:, o + 107:o + 109] = inp["ln_c_b"][l].reshape(2, 128).T
    vec[:, DEPTH * NVL:DEPTH * NVL + 8] = inp["g_final"].reshape(8, 128).T
    lnb = np.ascontiguousarray(np.broadcast_to(
        np.stack([inp["ln_b_g"], inp["ln_b_b"]], axis=1)[:, None, :, :], (DEPTH, 128, 2, 384))).astype(f)
    ws = inp["w_s"]
    wsT = np.ascontiguousarray(ws.transpose(0, 3, 1, 2))
    sub = ws[:, :, :8, :8]
    wsA = np.ascontiguousarray(np.repeat(sub.transpose(0, 3, 1, 2)[:, :, None, :, :], 16, axis=2)
                               .reshape(DEPTH, 128, 6, 8))
    bs = inp["b_s"].reshape(DEPTH, 3, 2, 128)
    bsP = np.ascontiguousarray(np.repeat(bs.transpose(0, 2, 1, 3)[:, :, None, :, :], 64, axis=2)
                               .reshape(DEPTH, 128, 3, 128))
    bsS = np.ascontiguousarray(np.repeat(bsP[:, :, :, :8, None], 16, axis=4).reshape(DEPTH, 128, 3, 128))
    shared = {"vec": vec, "lnb": lnb, "wsT": wsT, "wsA": wsA, "bsP": bsP, "bsS": bsS}
    for k in ("w_in", "w_out", "w_gate", "w_up", "w_down", "w_ple_gate", "w_ple_proj"):
        shared[k] = np.ascontiguousarray(inp[k], dtype=f)
    maps = []
    for c in range(NCORE):
        b0 = c * 16
        xs = inp["x_sample"][b0:b0 + 16].transpose(1, 0, 2).reshape(128, D)
        xin = np.ascontiguousarray(np.concatenate([inp["x_prompt"][c], xs], axis=0), dtype=f)
        ps_ = inp["p_sample"][:, b0:b0 + 16].transpose(0, 2, 1, 3).reshape(DEPTH, 128, 256)
        pin = np.ascontiguousarray(np.concatenate([inp["p_prompt"][:, c], ps_], axis=1), dtype=f)
        sa = np.ascontiguousarray(inp["state_conv_a"][:, b0:b0 + 16].transpose(0, 2, 1, 3).reshape(DEPTH, 32, 384), dtype=f)
        sc = np.ascontiguousarray(inp["state_conv_c"][:, b0:b0 + 16].transpose(0, 2, 1, 3).reshape(DEPTH, 480, 256), dtype=f)
        m = dict(shared)
        m.update({"xin": xin, "pin": pin, "sa": sa, "sc": sc})
        maps.append(m)
    return maps


def kernel(**inputs):
    inp = {k: np.asarray(v) for k, v in inputs.items()}
    if "nc" not in _NC_CACHE:
        _NC_CACHE["nc"] = build_program()
    nc = _NC_CACHE["nc"]
    maps = _host_layout(inp)
    res = run_bass_kernel_spmd(nc, maps, core_ids=list(range(NCORE)))
    R = res.results
    f = np.float32
    y_prompt = np.stack([np.asarray(R[c]["y"])[:2048] for c in range(NCORE)]).astype(f)
    y_sample = np.concatenate([np.asarray(R[c]["y"])[2048:].reshape(8, 16, D).transpose(1, 0, 2)
                               for c in range(NCORE)], axis=0).astype(f)
    na_p = np.stack([np.asarray(R[c]["nap"]) for c in range(NCORE)], axis=1).astype(f)
    na_s = np.concatenate([np.asarray(R[c]["nas"]).reshape(DEPTH, 2, 16, 384).transpose(0, 2, 1, 3)
                           for c in range(NCORE)], axis=1).astype(f)
    nc_p = np.stack([np.asarray(R[c]["ncp"]) for c in range(NCORE)], axis=1).astype(f)
    nc_s = np.concatenate([np.asarray(R[c]["ncs"]).reshape(DEPTH, 30, 16, 256).transpose(0, 2, 1, 3)
                           for c in range(NCORE)], axis=1).astype(f)
    nv_p = np.stack([np.asarray(R[c]["nvp"]) for c in range(NCORE)], axis=1).astype(f)
    nv_s = np.concatenate([np.asarray(R[c]["nvs"]).reshape(DEPTH, 8, 16, 384).transpose(0, 2, 1, 3)
                           for c in range(NCORE)], axis=1).astype(f)
    return (y_prompt, y_sample, na_p, na_s, nc_p, nc_s, nv_p, nv_s)
```

```python
import numpy as np
from contextlib import ExitStack
from collections import deque
import concourse.bass as bass
import concourse.mybir as mybir
from concourse.bass_utils import run_bass_kernel_spmd

F32 = mybir.dt.float32
BF16 = mybir.dt.bfloat16
ALU = mybir.AluOpType
AF = mybir.ActivationFunctionType
AX = mybir.AxisListType

DEPTH = 4
D = 1024
DIN = 2432
DFF = 2816
NCORE = 8
EPS = 1e-6
TC = 1152
XS0 = 1026
GS0 = 1054
XAW = 1186
GW = 1662
NS = 5
SLOTW = 3072
NVL = 109
NV = NVL * DEPTH + 8
ENGS = ("pe", "act", "dve", "pool", "sp")
NDS = 8


class _Stop(Exception):
    pass


_DBG = None


class Sched:
    def __init__(self):
        self.ops = {e: [] for e in ENGS}
        self.cnt = {}
        self.known = {e: {} for e in ENGS}
        self.lastw = {}
        self.readers = {}
        self.stopped = False

    def op(self, eng, fn, reads=(), writes=(), fence=(), sem=None, inc=1, serialize=True):
        if self.stopped:
            return 0
        waits = {}

        def need(sv):
            if sv is not None and sv[1] > waits.get(sv[0], 0):
                waits[sv[0]] = sv[1]
        for k in reads:
            need(self.lastw.get(k))
        for k in list(writes) + list(fence):
            if serialize:
                need(self.lastw.get(k))
            for s, v in self.readers.get(k, {}).items():
                need((s, v))
        semname = sem or eng
        if sem is not None and serialize:
            need((semname, self.cnt.get(semname, 0)))
        kn = self.known[eng]
        wl = []
        for s, v in waits.items():
            if v <= 0 or (eng == "pe" and s == "pe"):
                continue
            if kn.get(s, 0) >= v:
                continue
            assert v <= self.cnt.get(s, 0), ("wait on a pending (unmaterialised) count", eng, s, v)
            kn[s] = v
            wl.append((s, v))
        if inc:
            self.cnt[semname] = self.cnt.get(semname, 0) + inc
            val = self.cnt[semname]
        else:
            val = self.cnt.get(semname, 0) + 1
        self.ops[eng].append((wl, fn, semname, inc))
        for k in writes:
            self.lastw[k] = (semname, val)
            self.readers[k] = {}
        for k in reads:
            self.readers.setdefault(k, {})[semname] = val
        return val


def build_program():
    nc = bass.Bass("TRN2", target_bir_lowering=False)

    def din(name, shape):
        return nc.dram_tensor(name, list(shape), F32, kind="ExternalInput").ap()

    def dout(name, shape):
        return nc.dram_tensor(name, list(shape), F32, kind="ExternalOutput").ap()

    xin = din("xin", [2176, D])
    pin = din("pin", [DEPTH, 2176, 256])
    sa = din("sa", [DEPTH, 32, 384])
    sc = din("sc", [DEPTH, 480, 256])
    vec = din("vec", [128, NV])
    lnb = din("lnb", [DEPTH, 128, 2, 384])
    wsT = din("wsT", [DEPTH, 128, 6, 128])
    wsA = din("wsA", [DEPTH, 128, 6, 8])
    bsP = din("bsP", [DEPTH, 128, 3, 128])
    bsS = din("bsS", [DEPTH, 128, 3, 128])
    w_in = din("w_in", [DEPTH, D, DIN])
    w_out = din("w_out", [DEPTH, D, D])
    w_gate = din("w_gate", [DEPTH, D, DFF])
    w_up = din("w_up", [DEPTH, D, DFF])
    w_down = din("w_down", [DEPTH, DFF, D])
    w_pg = din("w_ple_gate", [DEPTH, D, D])
    w_pp = din("w_ple_proj", [DEPTH, 256, D])

    y_o = dout("y", [2176, D])
    nap_o = dout("nap", [DEPTH, 2, 384])
    nas_o = dout("nas", [DEPTH, 32, 384])
    ncp_o = dout("ncp", [DEPTH, 30, 256])
    ncs_o = dout("ncs", [DEPTH, 480, 256])
    nvp_o = dout("nvp", [DEPTH, 128, 384])
    nvs_o = dout("nvs", [DEPTH, 128, 384])

    S = Sched()
    es = ExitStack()
    with es:
        def sb(name, shape, dt):
            return es.enter_context(nc.sbuf_tensor(name, list(shape), dt))

        H = sb("H", [128, 8, TC], F32)
        NRM = sb("NRM", [128, 8, TC], BF16)
        Y = sb("Y", [128, 8, TC], F32)
        XA = sb("XA", [128, 2, XAW], F32)
        XAH = sb("XAH", [128, 3, 32], F32)
        GB = sb("GB", [128, 2, GW], BF16)
        GFS = sb("GFS", [128, 2, 608], F32)
        GFT = sb("GFT", [128, 2, 30], F32)
        DG = sb("DG", [128, 31, 128], BF16)
        VN = sb("VN", [128, 9, 384], BF16)
        VNF = sb("VNF", [128, 384], F32)
        VG = sb("VG", [128, 3, 384], F32)
        BST = sb("BST", [128, 3, 8], F32)
        SLOT = sb("SLOT", [128, NS, SLOTW], BF16)
        PT = sb("PT", [128, 2, TC], BF16)
        STG = sb("STG", [128, 2, 1024], F32)
        SQ = sb("SQ", [128, 4, 512], BF16)
        RB = sb("RB", [128, 3, 512], F32)
        TMP = sb("TMP", [128, 3, 512], F32)
        VEC = sb("VEC", [128, NV], F32)
        LNB = sb("LNB", [128, 2, 384], F32)
        WSA = sb("WSA", [128, 6, 8], F32)
        MP = sb("MP", [128, 6, 128], BF16)
        MS = sb("MS", [128, 6, 128], BF16)
        BSP = sb("BSP", [128, 3, 128], F32)
        BSS = sb("BSS", [128, 3, 128], F32)
        IDF = sb("IDF", [128, 128], F32)
        ONES = sb("ONES", [128, 128], BF16)
        DM = sb("DM", [128, 8, 16], F32)
        EQ = sb("EQ", [128, 16], F32)
        EPSC = sb("EPSC", [128, 1], F32)
        NEGH = sb("NEGH", [128, 1], F32)
        PSTG = sb("PSTG", [128, 4, 256], F32)
        CARA = sb("CARA", [128, DEPTH, 3, 2], F32)
        CARG = sb("CARG", [128, DEPTH, 2, 30], F32)
        PS = es.enter_context(nc.psum_tensor("PS", [128, 8, 512], F32))
        ACTB = Y[:].bitcast(BF16)[:, 0:4, :].rearrange("p a (b c) -> p (a b) c", c=TC)

        semnames = list(ENGS) + [f"d{i}" for i in range(NDS)] + [f"w{i}" for i in range(NS)]
        SEM = {n: es.enter_context(nc.semaphore("s_" + n)) for n in semnames}

        st = {"bank": 0, "sbank": 0, "ds": 0, "stg": 0, "sq": 0, "rb": 0, "tmp": 0, "pstg": 0, "pumpk": 3}

        def newbank():
            b = st["bank"]
            st["bank"] = (b + 1) % 6
            return b

        def statbank():
            b = st["sbank"]
            st["sbank"] = (b + 1) % 2
            return 6 + b

        bgq = deque()
        bg_done = set()
        bg_ctr = [0]

        def bg_add(steps):
            hid = bg_ctr[0]
            bg_ctr[0] += 1
            for i, fn in enumerate(steps):
                bgq.append((hid, fn, i == len(steps) - 1))
            return hid

        def pump(k=1):
            for _ in range(k):
                if not bgq:
                    return
                hid, fn, last = bgq.popleft()
                fn()
                if last:
                    bg_done.add(hid)

        def ensure(hid):
            while hid is not None and hid not in bg_done:
                assert bgq
                pump(1)

        def flush():
            while bgq:
                pump(1)

        def ring(name, n):
            i = st[name]
            st[name] = (i + 1) % n
            return i

        def dma(out, in_, reads=(), writes=(), fence=()):
            i = ring("ds", NDS)
            S.op("sp", lambda e: e.dma_start(out=out, in_=in_), reads=reads, writes=writes,
                 fence=fence, sem=f"d{i}", inc=16)

        def vcol(l, off, j=0):
            c = l * NVL + off + j
            return VEC[:, c:c + 1]

        plan = []

        def wplan():
            for hf in range(2):
                for l in range(DEPTH):
                    wi = w_in[l].rearrange("(kc p) m -> p kc m", p=128)
                    plan.append(("U", [(0, wi[:, :, 1152:1536])], 384))
                    plan.append(("V", [(0, wi[:, :, 1536:1920])], 384))
                    plan.append(("CV", [(0, wi[:, :, 1920:2176])], 256))
                    plan.append(("CG", [(0, wi[:, :, 2176:2432])], 256))
                    for j in range(3):
                        plan.append(("A", [(0, wi[:, :, j * 128:(j + 1) * 128]),
                                           (128, wi[:, :, 768 + j * 128:768 + (j + 1) * 128]),
                                           (256, wi[:, :, 384 + j * 128:384 + (j + 1) * 128])], 384))
                    wo = w_out[l].rearrange("(kc p) m -> p kc m", p=128)
                    for (m0, mw) in ((0, 384), (384, 384), (768, 256)):
                        plan.append(("O", [(0, wo[:, :, m0:m0 + mw])], mw))
                    wg = w_gate[l].rearrange("(kc p) m -> p kc m", p=128)
                    wu = w_up[l].rearrange("(kc p) m -> p kc m", p=128)
                    wd = w_down[l].rearrange("(kc p) m -> p kc m", p=128)
                    for pc in range(3):
                        j0 = (0, 8, 15)[pc]
                        nj = (8, 7, 7)[pc]
                        for (a, w_) in ((0, 3), (3, 3), (6, nj - 6)):
                            c0 = (j0 + a) * 128
                            plan.append(("FG", [(0, wg[:, :, c0:c0 + w_ * 128])], w_ * 128))
                            plan.append(("FU", [(0, wu[:, :, c0:c0 + w_ * 128])], w_ * 128))
                        for (a, w_) in ((0, 3), (3, 3), (6, nj - 6)):
                            plan.append(("FD", [(0, wd[:, j0 + a:j0 + a + w_, :])], 1024))
                    wpg = w_pg[l].rearrange("(kc p) m -> p kc m", p=128)
                    for ip_, (m0, mw) in enumerate(((0, 384), (384, 384), (768, 256))):
                        plan.append(("PG", [(0, wpg[:, :, m0:m0 + mw])], mw))
                        if ip_ == 0:
                            plan.append(("PP", [(0, w_pp[l].rearrange("(kc p) m -> p kc m", p=128))], 1024))
        wplan()
        wst = {"next_load": 0, "next_acq": 0, "released": 0}

        def slot_view(si, nk, w):
            return SLOT[:, si, 0:nk * w].rearrange("p (k m) -> p k m", k=nk)

        def try_load():
            while wst["next_load"] < len(plan) and wst["next_load"] - NS < wst["released"]:
                i = wst["next_load"]
                tag, parts, w = plan[i]
                si = i % NS
                for pi_, (off, src) in enumerate(parts):
                    nk = src.shape[1]
                    dst = slot_view(si, nk, w)[:, :, off:off + src.shape[2]]
                    S.op("pool", lambda e, dst=dst, src=src: e.dma_start(out=dst, in_=src),
                         writes=[("slot", si)], sem=f"w{si}", inc=16, serialize=(pi_ == 0))
                wst["next_load"] += 1

        def wacq(tag):
            i = wst["next_acq"]
            assert plan[i][0] == tag, (plan[i][0], tag)
            wst["next_acq"] += 1
            try_load()
            assert wst["next_load"] > i
            si = i % NS
            nk = plan[i][1][0][1].shape[1]
            return si, slot_view(si, nk, plan[i][2])

        def wrel(n=1):
            wst["released"] += n
            try_load()

        dma(VEC[:], vec, writes=[("VEC",)])
        S.op("pool", lambda e: e.memset(ONES[:], 1.0), writes=[("ONES",)])
        S.op("pool", lambda e: e.memset(EPSC[:], EPS), writes=[("EPSC",)])
        S.op("pool", lambda e: e.memset(NEGH[:], -0.5), writes=[("NEGH",)])
        S.op("pool", lambda e: e.memset(IDF[:], 1.0), writes=[("IDF",)])
        S.op("pool", lambda e: e.affine_select(IDF[:], IDF[:], [[-1, 128]], ALU.is_equal, 0.0,
                                               base=0, channel_multiplier=1),
             writes=[("IDF",)])
        S.op("dve", lambda e: e.tensor_reduce(EQ[:], IDF[:].rearrange("p (t q) -> p q t", q=16),
                                              AX.X, ALU.add),
             reads=[("IDF",)], writes=[("EQ",)])
        for t in range(8):
            S.op("dve", lambda e, t=t: e.tensor_copy(DM[:, t, :], EQ[:]), reads=[("EQ",)], writes=[("DM",)])
        S.op("pool", lambda e: e.affine_select(DM[:], DM[:], [[16, 8], [0, 16]], ALU.is_ge, 0.0,
                                               base=15, channel_multiplier=-1),
             reads=[("DM",)], writes=[("DM",)])

        def mmgroup(out_ap, bank, pairs):
            n = len(pairs)
            for i, (l_, r_, rk) in enumerate(pairs):
                S.op("pe", lambda e, l_=l_, r_=r_, i=i: e.matmul(out_ap, l_, r_, start=(i == 0), stop=(i == n - 1)),
                     reads=rk, writes=[("ps", bank)], inc=(1 if i == n - 1 else 0))

        def rstd_from_sum(bank, n, inv_n):
            ri = ring("rb", 3)
            rb = RB[:, ri, 0:n]
            S.op("act", lambda e: e.activation(rb, PS[:, bank, 0:n], AF.Ln, bias=EPSC[:, 0:1], scale=inv_n),
                 reads=[("EPSC",)], writes=[("ps", bank), ("RB", ri)])
            S.op("act", lambda e: e.activation(rb, rb, AF.Exp, scale=-0.5),
                 writes=[("RB", ri)])
            return ri, rb

        def rmsnorm_block(bi, c0, n, src, srckey, chunks, gcol, dst, dstkey, inv_n, extra_reads=(), fence=()):
            bank = newbank()
            nchk = len(chunks)
            for i, c in enumerate(chunks):
                qi = ring("sq", 4)
                sq = SQ[:, qi, 0:n]
                S.op("act", lambda e, c=c, sq=sq: e.activation(sq, src[:, c, c0:c0 + n], AF.Square),
                     reads=[(srckey, c, bi)] + list(extra_reads), writes=[("SQ", qi)])
                S.op("pe", lambda e, sq=sq, i=i: e.matmul(PS[:, bank, 0:n], ONES[:], sq,
                                                          start=(i == 0), stop=(i == nchk - 1)),
                     reads=[("SQ", qi), ("ONES",)], writes=[("ps", bank)], inc=1)
            ri, rb = rstd_from_sum(bank, n, inv_n)
            for i, c in enumerate(chunks):
                eng = "dve"
                S.op(eng, lambda e, c=c: e.scalar_tensor_tensor(dst[:, c, c0:c0 + n], src[:, c, c0:c0 + n],
                                                                 gcol(c), rb, ALU.mult, ALU.mult),
                     reads=[(srckey, c, bi), ("RB", ri), ("VEC",)] + list(extra_reads),
                     writes=[(dstkey, c, bi)], fence=fence)

        def rmsnorm_async(bi, c0, n, src, srckey, chunks, gcol, dst, dstkey, inv_n, extra_reads=(), fence=(), split=False):
            stt = {}
            nchk = len(chunks)

            LAG = 3

            def s_chunk(i):
                def f():
                    if i < nchk:
                        c = chunks[i]
                        qi = ring("sq", 4)
                        sq = SQ[:, qi, 0:n]
                        stt[("sq", i)] = (qi, sq)
                        S.op("act", lambda e: e.activation(sq, src[:, c, c0:c0 + n], AF.Square),
                             reads=[(srckey, c, bi)] + list(extra_reads), writes=[("SQ", qi)])
                    k = i - LAG
                    if k >= 0:
                        if k == 0:
                            stt["bank"] = statbank()
                        bank = stt["bank"]
                        qj, sqj = stt[("sq", k)]
                        S.op("pe", lambda e: e.matmul(PS[:, bank, 0:n], ONES[:], sqj, start=(k == 0), stop=(k == nchk - 1)),
                             reads=[("SQ", qj), ("ONES",)], writes=[("ps", bank)], inc=1)
                return f

            def s_rstd():
                stt["ri"], stt["rb"] = rstd_from_sum(stt["bank"], n, inv_n)

            def s_mul(cs):
                def f():
                    ri, rb = stt["ri"], stt["rb"]
                    for c in cs:
                        S.op("dve", lambda e, c=c: e.scalar_tensor_tensor(dst[:, c, c0:c0 + n], src[:, c, c0:c0 + n],
                                                                           gcol(c), rb, ALU.mult, ALU.mult),
                             reads=[(srckey, c, bi), ("RB", ri), ("VEC",)] + list(extra_reads),
                             writes=[(dstkey, c, bi)], fence=fence)
                return f
            steps = [s_chunk(i) for i in range(nchk + LAG)] + [s_rstd]
            half = (nchk + 1) // 2
            msteps = [s_mul(chunks[:half]), s_mul(chunks[half:])]
            if split:
                bg_add(steps)
                return lambda: bg_add(msteps)
            return bg_add(steps + msteps)

        def transpose_in(src_rows_ap, nrows, ncolchunks, evac):
            pass

        def dbg(*tag):
            if _DBG == tag:
                S.stopped = True

        for hf in range(2):
            if hf == 0:
                blocks = [(0, 512, "P"), (512, 512, "P"), (1024, 128, "S")]
                tiles = [(i * 128, i * 128, "P") for i in range(8)] + [(1024, 2048, "S")]
            else:
                blocks = [(0, 384, "P"), (384, 384, "P"), (768, 256, "P")]
                tiles = [(i * 128, 1024 + i * 128, "P") for i in range(8)]
            lastp = max(i for i, b in enumerate(blocks) if b[2] == "P")
            sblk = len(blocks) - 1
            nblk = len(blocks)
            ntile = len(tiles)
            nrm_h = {}

            def blk_of_col(c):
                for bi, (c0, n, _) in enumerate(blocks):
                    if c0 <= c < c0 + n:
                        return bi
                raise AssertionError

            for (col0, row0, _) in tiles:
                si = ring("stg", 2)
                bi = blk_of_col(col0)
                for hh_ in range(2):
                    dma(STG[:, si, hh_ * 512:(hh_ + 1) * 512], xin[row0:row0 + 128, hh_ * 512:(hh_ + 1) * 512],
                        writes=[("STG", si, hh_)], fence=[("STG", si)])
                for half8 in range(2):
                    bank = newbank()
                    for k4 in range(4):
                        kc = half8 * 4 + k4
                        S.op("pe", lambda e, kc=kc, k4=k4, bank=bank, si=si: e.transpose(
                            PS[:, bank, k4 * 128:(k4 + 1) * 128], STG[:, si, kc * 128:(kc + 1) * 128], IDF[:]),
                            reads=[("STG", si, half8), ("STG", si), ("IDF",)], writes=[("ps", bank)], inc=(1 if k4 == 3 else 0))
                    S.op("act" if half8 == 0 else "dve",
                         (lambda e, bank=bank, half8=half8, col0=col0: e.activation(
                             H[:, half8 * 4:half8 * 4 + 4, col0:col0 + 128],
                             PS[:, bank, :].rearrange("p (k c) -> p k c", k=4), AF.Copy)) if half8 == 0 else
                         (lambda e, bank=bank, half8=half8, col0=col0: e.tensor_copy(
                             H[:, half8 * 4:half8 * 4 + 4, col0:col0 + 128],
                             PS[:, bank, :].rearrange("p (k c) -> p k c", k=4))),
                         writes=[("ps", bank)] + [("H", half8 * 4 + k, bi) for k in range(4)])

            for l in range(DEPTH):
                dma(LNB[:], lnb[l], writes=[("LNB",)])
                WST = VG[:, 0:2, :].rearrange("p a (b t) -> p (a b) t", t=128)
                dma(WST, wsT[l], writes=[("VG", 0), ("VG", 1)])
                dma(WSA[:], wsA[l], writes=[("WSA",)])
                dma(BSP[:], bsP[l], writes=[("BSP",)])
                dma(BSS[:], bsS[l], writes=[("BSS",)])
                S.op("pool", lambda e: e.affine_select(MP[:], WST, [[0, 6], [1, 128]], ALU.is_ge, 0.0,
                                                       base=0, channel_multiplier=-1),
                     reads=[("VG", 0), ("VG", 1)], writes=[("MP",)])
                if hf == 0:
                    for h in range(6):
                        S.op("pool", lambda e, h=h: e.tensor_tensor(
                            MS[:, h, :].rearrange("p (t q) -> p t q", q=16),
                            WSA[:, h, :].unsqueeze(2).to_broadcast([128, 8, 16]), DM[:], ALU.mult),
                            reads=[("WSA",), ("DM",)], writes=[("MS",)])
                if hf == 0:
                    for j in range(2):
                        S.op("pool", lambda e, j=j: e.memset(GB[:, j, 0:30], 0.0), writes=[("Gh", j)])
                    si = ring("stg", 2)
                    dma(STG[0:32, si, 0:384], sa[l], writes=[("STG", si)])
                    bank = newbank()
                    for j in range(3):
                        S.op("pe", lambda e, j=j, bank=bank, si=si: e.transpose(
                            PS[:, bank, j * 32:(j + 1) * 32], STG[0:32, si, j * 128:(j + 1) * 128], IDF[0:32, 0:32]),
                            reads=[("STG", si), ("IDF",)], writes=[("ps", bank)], inc=(1 if j == 2 else 0))
                    S.op("dve", lambda e, bank=bank: e.tensor_copy(
                        XAH[:, :, :], PS[:, bank, 0:96].rearrange("p (j c) -> p j c", j=3)),
                        writes=[("ps", bank), ("XAH",)])
                    for rt in range(4):
                        nr = 128 if rt < 3 else 96
                        si = ring("stg", 2)
                        dma(STG[0:nr, si, 0:256], sc[l, rt * 128:rt * 128 + nr, :], writes=[("STG", si)])
                        bank = newbank()
                        for j in range(2):
                            S.op("pe", lambda e, j=j, bank=bank, si=si, nr=nr: e.transpose(
                                PS[:, bank, j * 128:j * 128 + nr], STG[0:nr, si, j * 128:(j + 1) * 128],
                                IDF[0:nr, 0:nr]),
                                reads=[("STG", si), ("IDF",)], writes=[("ps", bank)], inc=(1 if j == 1 else 0))
                        S.op("dve", lambda e, bank=bank, rt=rt, nr=nr: e.tensor_copy(
                            GFS[:, :, rt * 128:rt * 128 + nr],
                            PS[:, bank, 0:256].rearrange("p (j c) -> p j c", j=2)[:, :, 0:nr]),
                            writes=[("ps", bank)] + [("GFS", j) for j in range(2)])
                    for j in range(2):
                        S.op("pool", lambda e, j=j: e.tensor_copy(GB[:, j, GS0:GS0 + 480], GFS[:, j, 0:480]),
                             reads=[("GFS", j)], writes=[("Gsh", j)])
                else:
                    for j in range(2):
                        S.op("pool", lambda e, l=l, j=j: e.tensor_copy(GB[:, j, 0:30], CARG[:, l, j, :]),
                             reads=[("CARG", l)], writes=[("Gh", j)])

                pt_h = {}
                pload, ptrans = [], []
                for ti_, (col0, row0, _) in enumerate(tiles):
                    pi = ti_ % 4

                    def pld(pi=pi, row0=row0, l=l):
                        dma(PSTG[:, pi, :], pin[l, row0:row0 + 128, :], writes=[("PSTG", pi)])
                    pload.append(pld)

                    def pstep(pi=pi, col0=col0):
                        bank = newbank()
                        for j in range(2):
                            S.op("pe", lambda e, j=j, bank=bank, pi=pi: e.transpose(
                                PS[:, bank, j * 128:(j + 1) * 128], PSTG[:, pi, j * 128:(j + 1) * 128], IDF[:]),
                                reads=[("PSTG", pi), ("IDF",)], writes=[("ps", bank)], inc=(1 if j == 1 else 0))
                        S.op("act", lambda e, bank=bank, col0=col0: e.activation(
                            PT[:, :, col0:col0 + 128], PS[:, bank, 0:256].rearrange("p (j c) -> p j c", j=2), AF.Copy),
                            writes=[("ps", bank), ("PT", col0 // 128)])
                    ptrans.append(pstep)
                for ti_ in range(min(4, ntile)):
                    bg_add([pload[ti_]])
                for ti_ in range(ntile):
                    pt_h[ti_] = bg_add([ptrans[ti_]])
                    if ti_ + 4 < ntile:
                        bg_add([pload[ti_ + 4]])

                if l == 0:
                    for bi, (c0, n, _) in enumerate(blocks):
                        nrm_h[bi] = rmsnorm_async(bi, c0, n, H, "H", list(range(8)), lambda c, l=l: vcol(l, 0, c),
                                                  NRM, "NRM", 1.0 / D)

                S.op("dve", lambda e, b0_=l * NVL + 41: e.tensor_tensor(
                    DG[:], IDF[:].unsqueeze(1).to_broadcast([128, 31, 128]),
                    VEC[:, b0_:b0_ + 62:2].unsqueeze(2).to_broadcast([128, 31, 128]), ALU.mult),
                    reads=[("IDF",), ("VEC",)], writes=[("DG",)])
                siu, wvu = wacq("U")
                si, wv = wacq("V")

                def u_block(bi):
                    c0, n, kind = blocks[bi]
                    ensure(nrm_h[bi])
                    for j in range(3):
                        bk = newbank()
                        mmgroup(PS[:, bk, 0:n], bk,
                                [(wvu[:, kc, j * 128:(j + 1) * 128], NRM[:, kc, c0:c0 + n],
                                  [("slot", siu), ("NRM", kc, bi)]) for kc in range(8)])
                        S.op("act", lambda e, bk=bk, j=j, c0=c0, n=n: e.activation(
                            Y[:, 3 + j, c0:c0 + n], PS[:, bk, 0:n], AF.Gelu),
                            reads=[("Yreg",)], writes=[("ps", bk), ("Y", 3 + j, bi)], fence=[("Areg",)])
                        pump(st["pumpk"])

                u_block(0)
                for tbi, tb in enumerate(range(0, ntile, 3)):
                    batch = list(range(tb, min(tb + 3, ntile)))
                    for ti_ in batch:
                        col0, row0, kind = tiles[ti_]
                        bi = blk_of_col(col0)
                        ensure(nrm_h[bi])
                        bk = newbank()
                        mmgroup(PS[:, bk, 0:384], bk,
                                [(NRM[:, kc, col0:col0 + 128], wv[:, kc, :],
                                  [("slot", si), ("NRM", kc, bi)]) for kc in range(8)])
                        gi = ti_ % 3
                        S.op("act", lambda e, bk=bk, gi=gi: e.activation(VG[:, gi, :], PS[:, bk, 0:384], AF.Gelu),
                             writes=[("ps", bk), ("VG", gi)])
                        S.op("dve", lambda e, gi=gi: e.bn_stats(BST[:, gi, 0:6], VG[:, gi, :]),
                             reads=[("VG", gi)], writes=[("BST", gi)])
                        S.op("dve", lambda e, gi=gi: e.bn_aggr(BST[:, gi, 6:8], BST[:, gi, 0:6]),
                             writes=[("BST", gi)])
                        pump(st["pumpk"])
                    if tbi + 1 < nblk:
                        u_block(tbi + 1)
                    for ti_ in batch:
                        col0, row0, kind = tiles[ti_]
                        gi = ti_ % 3
                        S.op("act", lambda e, gi=gi: e.activation(BST[:, gi, 7:8], BST[:, gi, 7:8], AF.Ln, bias=EPSC[:, 0:1]),
                             reads=[("EPSC",)], writes=[("BST", gi)])
                        S.op("act", lambda e, gi=gi: e.activation(BST[:, gi, 7:8], BST[:, gi, 7:8], AF.Exp, scale=-0.5),
                             writes=[("BST", gi)])
                        S.op("dve", lambda e, gi=gi: e.tensor_scalar(VG[:, gi, :], VG[:, gi, :], BST[:, gi, 6:7],
                                                                     BST[:, gi, 7:8], ALU.subtract, ALU.mult),
                             reads=[("BST", gi)], writes=[("VG", gi)])
                        S.op("pool", lambda e, gi=gi: e.tensor_tensor(VG[:, gi, :], VG[:, gi, :], LNB[:, 0, :], ALU.mult),
                             reads=[("LNB",)], writes=[("VG", gi)])
                        is_out = (hf == 0 and kind == "S") or (hf == 1 and ti_ == ntile - 1)
                        if is_out:
                            S.op("pool", lambda e, gi=gi: e.tensor_tensor(VNF[:], VG[:, gi, :], LNB[:, 1, :], ALU.add),
                                 reads=[("LNB",), ("VG", gi)], writes=[("VNF",)])
                            S.op("pool", lambda e, ti_=ti_: e.tensor_copy(VN[:, ti_, :], VNF[:]),
                                 reads=[("VNF",)], writes=[("VN", ti_)])
                            dma((nvs_o if kind == "S" else nvp_o)[l], VNF[:], reads=[("VNF",)])
                        else:
                            S.op("pool", lambda e, gi=gi, ti_=ti_: e.tensor_tensor(VN[:, ti_, :], VG[:, gi, :], LNB[:, 1, :], ALU.add),
                                 reads=[("LNB",), ("VG", gi)], writes=[("VN", ti_)])
                assert (ntile + 2) // 3 >= nblk - 1
                wrel(2)
                siv, wvv = wacq("CV")
                sig, wvg = wacq("CG")
                for bi, (c0, n, kind) in enumerate(blocks):
                    for j in range(2):
                        bv, bg_ = newbank(), newbank()
                        mmgroup(PS[:, bv, 0:n], bv,
                                [(wvv[:, kc, j * 128:(j + 1) * 128], NRM[:, kc, c0:c0 + n],
                                  [("slot", siv), ("NRM", kc, bi)]) for kc in range(8)])
                        mmgroup(PS[:, bg_, 0:n], bg_,
                                [(wvg[:, kc, j * 128:(j + 1) * 128], NRM[:, kc, c0:c0 + n],
                                  [("slot", sig), ("NRM", kc, bi)]) for kc in range(8)])
                        ti = ring("tmp", 3)
                        tmp = TMP[:, ti, 0:n]
                        S.op("act", lambda e, bg_=bg_, tmp=tmp, n=n: e.activation(tmp, PS[:, bg_, 0:n], AF.Sigmoid),
                             writes=[("ps", bg_), ("TMP", ti)])
                        gc = (30 + c0) if kind == "P" else (GS0 + 480)
                        S.op("dve", lambda e, bv=bv, tmp=tmp, n=n, gc=gc, j=j: e.tensor_tensor(
                            GB[:, j, gc:gc + n], tmp, PS[:, bv, 0:n], ALU.mult),
                            reads=[("TMP", ti)], writes=[("ps", bv), ("G", j, bi)])
                        if kind == "P" and c0 + n == 1024:
                            S.op("dve", lambda e, bv=bv, tmp=tmp, n=n, j=j: e.tensor_tensor(
                                GFT[:, j, :], tmp[:, n - 30:n], PS[:, bv, n - 30:n], ALU.mult),
                                reads=[("TMP", ti)], writes=[("ps", bv), ("GFT", j)])
                        if kind == "S":
                            S.op("dve", lambda e, bv=bv, tmp=tmp, n=n, j=j: e.tensor_tensor(
                                GFS[:, j, 480:608], tmp, PS[:, bv, 0:n], ALU.mult),
                                reads=[("TMP", ti)], writes=[("ps", bv), ("GFS", j)])
                        pump(st["pumpk"])
                wrel(2)
                if hf == 0:
                    for j in range(2):
                        S.op("pool", lambda e, l=l, j=j: e.tensor_copy(CARG[:, l, j, :], GFT[:, j, :]),
                             reads=[("GFT", j)], writes=[("CARG", l)])
                    for rt in range(4):
                        nr = 128 if rt < 3 else 96
                        bank = newbank()
                        for j in range(2):
                            S.op("pe", lambda e, j=j, bank=bank, rt=rt, nr=nr: e.transpose(
                                PS[0:nr, bank, j * 128:(j + 1) * 128],
                                GFS[:, j, 128 + rt * 128:128 + rt * 128 + nr], IDF[:]),
                                reads=[("GFS", j), ("IDF",)], writes=[("ps", bank)], inc=(1 if j == 1 else 0))
                        si_ = ring("stg", 2)
                        S.op("dve", lambda e, bank=bank, si_=si_, nr=nr: e.tensor_copy(STG[0:nr, si_, 0:256], PS[0:nr, bank, 0:256]),
                             writes=[("ps", bank), ("STG", si_)])
                        dma(ncs_o[l, rt * 128:rt * 128 + nr, :], STG[0:nr, si_, 0:256], reads=[("STG", si_)])
                else:
                    bank = newbank()
                    for j in range(2):
                        S.op("pe", lambda e, j=j, bank=bank: e.transpose(
                            PS[0:30, bank, j * 128:(j + 1) * 128], GFT[:, j, :], IDF[:]),
                            reads=[("GFT", j), ("IDF",)], writes=[("ps", bank)], inc=(1 if j == 1 else 0))
                    si_ = ring("stg", 2)
                    S.op("dve", lambda e, bank=bank, si_=si_: e.tensor_copy(STG[0:30, si_, 0:256], PS[0:30, bank, 0:256]),
                         writes=[("ps", bank), ("STG", si_)])
                    dma(ncp_o[l], STG[0:30, si_, 0:256], reads=[("STG", si_)])

                def build_dg(j):
                    b0_ = l * NVL + 41 + j
                    S.op("dve", lambda e, b0_=b0_: e.tensor_tensor(
                        DG[:], IDF[:].unsqueeze(1).to_broadcast([128, 31, 128]),
                        VEC[:, b0_:b0_ + 62:2].unsqueeze(2).to_broadcast([128, 31, 128]), ALU.mult),
                        reads=[("IDF",), ("VEC",)], writes=[("DG",)])

                def conv31(j):
                    for bi, (c0, n, kind) in enumerate(blocks):
                        if kind == "P":
                            taps = [c0 + k for k in range(31)]
                            hk = [("G", j, bi - 1)] if bi > 0 else [("Gh", j)]
                        else:
                            taps = [GS0 + 16 * k for k in range(31)]
                            hk = [("Gsh", j)]
                        bk = newbank()
                        mmgroup(PS[:, bk, 0:n], bk,
                                [(DG[:, k, :], GB[:, j, taps[k]:taps[k] + n], [("DG",), ("G", j, bi)] + hk)
                                 for k in range(31)])
                        S.op("act", lambda e, l=l, j=j, bk=bk, c0=c0, n=n: e.activation(
                            Y[:, 6 + j, c0:c0 + n], PS[:, bk, 0:n], AF.Identity, bias=vcol(l, 103, j)),
                            reads=[("VEC",), ("Yreg",)], writes=[("ps", bk), ("Y", 6 + j, bi)], fence=[("Areg",)])
                        pump(st["pumpk"])

                def lnc_async(bi, c0, n):
                    stt = {}

                    def s_act(js):
                        def f():
                            for i in js:
                                func = AF.Copy if i < 2 else AF.Square
                                j = i % 2
                                qi = ring("sq", 4)
                                sq = SQ[:, qi, 0:n]
                                stt[("sq", i)] = (qi, sq)
                                S.op("act", lambda e, sq=sq, j=j, func=func: e.activation(sq, Y[:, 6 + j, c0:c0 + n], func),
                                     reads=[("Y", 6 + j, bi)], writes=[("SQ", qi)])
                        return f

                    def s_mm(ks):
                        def f():
                            for k in ks:
                                if k == 0:
                                    stt["bm"], stt["bq"] = statbank(), statbank()
                                bk_ = stt["bm"] if k < 2 else stt["bq"]
                                qj, sqj = stt[("sq", k)]
                                S.op("pe", lambda e, bk_=bk_, sqj=sqj, k=k: e.matmul(PS[:, bk_, 0:n], ONES[:], sqj, start=(k % 2 == 0), stop=(k % 2 == 1)),
                                     reads=[("SQ", qj), ("ONES",)], writes=[("ps", bk_)], inc=1)
                        return f

                    def s_fin():
                        bm, bq = stt["bm"], stt["bq"]
                        mi = ring("rb", 3)
                        mb = RB[:, mi, 0:n]
                        S.op("act", lambda e: e.activation(mb, PS[:, bm, 0:n], AF.Copy, scale=1.0 / 256),
                             writes=[("ps", bm), ("RB", mi)])
                        vi = ring("rb", 3)
                        vb = RB[:, vi, 0:n]
                        S.op("dve", lambda e: e.tensor_tensor(vb, mb, mb, ALU.mult),
                             reads=[("RB", mi)], writes=[("RB", vi)])
                        S.op("dve", lambda e: e.scalar_tensor_tensor(vb, PS[:, bq, 0:n], 1.0 / 256, vb, ALU.mult, ALU.subtract),
                             writes=[("ps", bq), ("RB", vi)])
                        S.op("act", lambda e: e.activation(vb, vb, AF.Ln, bias=EPSC[:, 0:1]),
                             reads=[("EPSC",)], writes=[("RB", vi)])
                        S.op("act", lambda e: e.activation(vb, vb, AF.Exp, scale=-0.5), writes=[("RB", vi)])
                        for j in range(2):
                            yc = Y[:, 6 + j, c0:c0 + n]
                            S.op("dve", lambda e, yc=yc: e.tensor_tensor(yc, yc, mb, ALU.subtract),
                                 reads=[("RB", mi)], writes=[("Y", 6 + j, bi)])
                            S.op("dve", lambda e, yc=yc: e.tensor_tensor(yc, yc, vb, ALU.mult),
                                 reads=[("RB", vi)], writes=[("Y", 6 + j, bi)])
                            S.op("act", lambda e, yc=yc, j=j, l=l: e.activation(yc, yc, AF.Silu, bias=vcol(l, 107, j), scale=vcol(l, 105, j)),
                                 reads=[("VEC",)], writes=[("Y", 6 + j, bi)])
                    return [s_act([0, 1]), s_act([2, 3])], [s_mm([0, 1]), s_mm([2, 3])], s_fin

                def groupA(j, hook_post=None, pre_block=None):
                    xj = j % 2
                    if hf == 0:
                        S.op("pool", lambda e, xj=xj: e.memset(XA[:, xj, 0:2], 0.0), writes=[("XAh", xj)])
                        S.op("pool", lambda e, xj=xj, j=j: e.tensor_copy(XA[:, xj, XS0:XS0 + 32], XAH[:, j, :]),
                             reads=[("XAH",)], writes=[("XAsh", xj)])
                    else:
                        S.op("pool", lambda e, l=l, xj=xj, j=j: e.tensor_copy(XA[:, xj, 0:2], CARA[:, l, j, :]),
                             reads=[("CARA", l, j)], writes=[("XAh", xj)])
                    si, wv = wacq("A")
                    for bi, (c0, n, kind) in enumerate(blocks):
                        if pre_block is not None:
                            pre_block(bi)
                        bx, bc, bb = newbank(), newbank(), newbank()
                        for (bk, off) in ((bx, 0), (bc, 128), (bb, 256)):
                            mmgroup(PS[:, bk, 0:n], bk,
                                    [(wv[:, kc, off:off + 128], NRM[:, kc, c0:c0 + n],
                                      [("slot", si), ("NRM", kc, bi)]) for kc in range(8)])
                        ti = ring("tmp", 3)
                        tmp = TMP[:, ti, 0:n]
                        S.op("act", lambda e, bx=bx, tmp=tmp, n=n: e.activation(tmp, PS[:, bx, 0:n], AF.Copy),
                             writes=[("ps", bx), ("TMP", ti)])
                        xc = (2 + c0) if kind == "P" else (XS0 + 32)
                        S.op("dve", lambda e, bc=bc, tmp=tmp, n=n, xc=xc, xj=xj: e.tensor_tensor(
                            XA[:, xj, xc:xc + n], tmp, PS[:, bc, 0:n], ALU.mult),
                            reads=[("TMP", ti)], writes=[("ps", bc), ("XA", xj, bi)])
                        if kind == "P":
                            taps = [c0 + k for k in range(3)]
                            hk = [("XA", xj, bi - 1)] if bi > 0 else [("XAh", xj)]
                        else:
                            taps = [XS0 + 16 * k for k in range(3)]
                            hk = [("XAsh", xj)]
                        t2 = ring("tmp", 3)
                        acc = TMP[:, t2, 0:n]
                        S.op("dve", lambda e, l=l, acc=acc, j=j, xj=xj, n=n, a=taps[2]: e.tensor_scalar(
                            acc, XA[:, xj, a:a + n], vcol(l, 32, 2 * 3 + j), None, ALU.mult),
                            reads=[("XA", xj, bi), ("VEC",)] + hk, writes=[("TMP", t2)])
                        for k in (1, 0):
                            S.op("dve", lambda e, l=l, acc=acc, j=j, xj=xj, n=n, a=taps[k], k=k: e.scalar_tensor_tensor(
                                acc, XA[:, xj, a:a + n], vcol(l, 32, k * 3 + j), acc, ALU.mult, ALU.add),
                                reads=[("XA", xj, bi), ("VEC",)] + hk, writes=[("TMP", t2)])
                        S.op("dve", lambda e, acc=acc, j=j, n=n, c0=c0, bb=bb: e.tensor_tensor(
                            Y[:, j, c0:c0 + n], acc, PS[:, bb, 0:n], ALU.mult),
                            reads=[("TMP", t2), ("Yreg",)], writes=[("ps", bb), ("Y", j, bi)], fence=[("Areg",)])
                        if hook_post is not None:
                            hook_post(bi)
                        pump(st["pumpk"])
                    wrel()
                    bank = newbank()
                    si_ = ring("stg", 2)
                    if hf == 0:
                        S.op("pool", lambda e, l=l, j=j, xj=xj: e.tensor_copy(CARA[:, l, j, :], XA[:, xj, 1024:1026]),
                             reads=[("XA", xj, lastp)], writes=[("CARA", l, j)])
                        S.op("pe", lambda e, xj=xj, bank=bank: e.transpose(
                            PS[0:32, bank, 0:128], XA[:, xj, XS0 + 128:XS0 + 160], IDF[:]),
                            reads=[("XA", xj, sblk), ("IDF",)], writes=[("ps", bank)])
                        S.op("dve", lambda e, bank=bank, si_=si_: e.tensor_copy(STG[0:32, si_, 0:128], PS[0:32, bank, 0:128]),
                             writes=[("ps", bank), ("STG", si_)])
                        dma(nas_o[l, :, j * 128:(j + 1) * 128], STG[0:32, si_, 0:128], reads=[("STG", si_)])
                    else:
                        S.op("pe", lambda e, xj=xj, bank=bank: e.transpose(
                            PS[0:2, bank, 0:128], XA[:, xj, 1024:1026], IDF[:]),
                            reads=[("XA", xj, lastp), ("IDF",)], writes=[("ps", bank)])
                        S.op("dve", lambda e, bank=bank, si_=si_: e.tensor_copy(STG[0:2, si_, 0:128], PS[0:2, bank, 0:128]),
                             writes=[("ps", bank), ("STG", si_)])
                        dma(nap_o[l, :, j * 128:(j + 1) * 128], STG[0:2, si_, 0:128], reads=[("STG", si_)])

                conv31(0)
                build_dg(1)
                groupA(0)
                conv31(1)
                lparts = [lnc_async(bi, c0, n) for bi, (c0, n, kind) in enumerate(blocks)]
                bg_add(lparts[0][0])
                for bi in range(nblk):
                    bg_add(lparts[bi][1])
                    if bi + 1 < nblk:
                        bg_add(lparts[bi + 1][0])
                    bg_add([lparts[bi][2]])
                groupA(1)
                def mix_block(bi):
                    c0, n, kind = blocks[bi]
                    for hp in range(3):
                        bk = newbank()
                        ntc = n // 128
                        for q4 in range(ntc):
                            ti_ = (c0 // 128) + q4
                            for hh in range(2):
                                h = 2 * hp + hh
                                M_ = MS if kind == "S" else MP
                                S.op("pe", lambda e, bk=bk, q4=q4, hh=hh, h=h, ti_=ti_, M_=M_: e.matmul(
                                    PS[hh * 64:(hh + 1) * 64, bk, q4 * 128:(q4 + 1) * 128],
                                    VN[:, ti_, h * 64:(h + 1) * 64], M_[:, h, :], start=True, stop=True),
                                    reads=[("VN", ti_), ("MS",) if kind == "S" else ("MP",)], writes=[("ps", bk)],
                                    inc=(1 if (q4 == ntc - 1 and hh == 1) else 0))
                        t2 = ring("tmp", 3)
                        tmp = TMP[:, t2, 0:n]
                        BS_ = BSS if kind == "S" else BSP
                        S.op("dve", lambda e, bk=bk, tmp=tmp, n=n, ntc=ntc, BS_=BS_, hp=hp: e.tensor_tensor(
                            tmp.rearrange("p (a t) -> p a t", a=ntc), PS[:, bk, 0:n].rearrange("p (a t) -> p a t", a=ntc),
                            BS_[:, hp, :].unsqueeze(1).to_broadcast([128, ntc, 128]), ALU.add),
                            reads=[("BSS",) if kind == "S" else ("BSP",)], writes=[("ps", bk), ("TMP", t2)])
                        S.op("dve", lambda e, tmp=tmp, c0=c0, n=n, hp=hp: e.tensor_tensor(
                            Y[:, 3 + hp, c0:c0 + n], tmp, Y[:, 3 + hp, c0:c0 + n], ALU.mult),
                            reads=[("TMP", t2), ("Yreg",)], writes=[("Y", 3 + hp, bi)])
                        pump(3)
                gn_h = {}
                gn0 = []

                def pre_block(bi):
                    mix_block(bi)
                    if bi == 0:
                        c0_, n_, _k = blocks[0]
                        gn0.extend(rmsnorm_async(0, c0_, n_, Y, "Y", list(chunks), lambda c, l=l: vcol(l, 24, c),
                                                 NRM, "NRM", inv_n, extra_reads=[("Yreg",)], split=True)
                                   for (chunks, inv_n) in (((3, 4, 5), 1.0 / 384), ((6, 7), 1.0 / 256)))

                def gn_hook(bi):
                    c0, n, kind = blocks[bi]
                    if bi == 0:
                        hs = [mk() for mk in gn0]
                        hs.append(rmsnorm_async(bi, c0, n, Y, "Y", [0, 1, 2], lambda c, l=l: vcol(l, 24, c),
                                                NRM, "NRM", 1.0 / 384, extra_reads=[("Yreg",)]))
                        gn_h[bi] = hs
                        return
                    gn_h[bi] = [rmsnorm_async(bi, c0, n, Y, "Y", list(chunks), lambda c, l=l: vcol(l, 24, c),
                                              NRM, "NRM", inv_n, extra_reads=[("Yreg",)])
                                for (chunks, inv_n) in (((3, 4, 5), 1.0 / 384), ((6, 7), 1.0 / 256), ((0, 1, 2), 1.0 / 384))]
                st["pumpk"] = 4
                groupA(2, hook_post=gn_hook, pre_block=pre_block)
                dbg(hf, l, 'B')

                osl = [wacq("O") for _ in range(3)]
                for bi, (c0, n, kind) in enumerate(blocks):
                    for h_ in gn_h[bi]:
                        ensure(h_)
                    for m in range(8):
                        si, wv = osl[m // 3]
                        mo = (m % 3) * 128
                        bk = newbank()
                        mmgroup(PS[:, bk, 0:n], bk,
                                [(wv[:, kc, mo:mo + 128], NRM[:, kc, c0:c0 + n],
                                  [("slot", si), ("NRM", kc, bi)]) for kc in range(8)])
                        S.op("dve", lambda e, bk=bk, m=m, c0=c0, n=n: e.tensor_tensor(
                            H[:, m, c0:c0 + n], H[:, m, c0:c0 + n], PS[:, bk, 0:n], ALU.add),
                            writes=[("ps", bk), ("H", m, bi)])
                        if bi == nblk - 1 and m in (2, 5):
                            wrel(1)
                        pump(st["pumpk"])
                    nrm_h[bi] = rmsnorm_async(bi, c0, n, H, "H", list(range(8)), lambda c, l=l: vcol(l, 8, c),
                                              NRM, "NRM", 1.0 / D)
                st["pumpk"] = 3
                wrel(1)
                dbg(hf, l, 'O')
                for pc in range(3):
                    nj = (8, 7, 7)[pc]
                    for (a, w_) in ((0, 3), (3, 3), (6, nj - 6)):
                        sg, wg_ = wacq("FG")
                        su, wu_ = wacq("FU")
                        for bi, (c0, n, kind) in enumerate(blocks):
                            ensure(nrm_h[bi])
                            for jj in range(w_):
                                jl = a + jj
                                bg_, bu = newbank(), newbank()
                                mmgroup(PS[:, bg_, 0:n], bg_,
                                        [(wg_[:, kc, jj * 128:(jj + 1) * 128], NRM[:, kc, c0:c0 + n],
                                          [("slot", sg), ("NRM", kc, bi)]) for kc in range(8)])
                                mmgroup(PS[:, bu, 0:n], bu,
                                        [(wu_[:, kc, jj * 128:(jj + 1) * 128], NRM[:, kc, c0:c0 + n],
                                          [("slot", su), ("NRM", kc, bi)]) for kc in range(8)])
                                ti = ring("tmp", 3)
                                tmp = TMP[:, ti, 0:n]
                                S.op("act", lambda e, bg_=bg_, tmp=tmp, n=n: e.activation(tmp, PS[:, bg_, 0:n], AF.Silu),
                                     writes=[("ps", bg_), ("TMP", ti)])
                                S.op("dve", lambda e, bu=bu, tmp=tmp, jl=jl, c0=c0, n=n: e.tensor_tensor(
                                    ACTB[:, jl, c0:c0 + n], tmp, PS[:, bu, 0:n], ALU.mult),
                                    reads=[("TMP", ti), ("Areg",)], writes=[("ps", bu), ("ACTB", jl, bi)], fence=[("Yreg",)])
                                pump(st["pumpk"])
                        wrel(2)
                    dsl = [wacq("FD"), wacq("FD"), wacq("FD")]
                    last = (pc == 2)
                    for bi, (c0, n, kind) in enumerate(blocks):
                        for m in range(8):
                            bk = newbank()
                            pairs = []
                            for jl in range(nj):
                                si, wv = dsl[jl // 3]
                                pairs.append((wv[:, jl % 3, m * 128:(m + 1) * 128], ACTB[:, jl, c0:c0 + n],
                                              [("slot", si), ("ACTB", jl, bi), ("Areg",)]))
                            mmgroup(PS[:, bk, 0:n], bk, pairs)
                            S.op("dve", lambda e, bk=bk, m=m, c0=c0, n=n: e.tensor_tensor(
                                H[:, m, c0:c0 + n], H[:, m, c0:c0 + n], PS[:, bk, 0:n], ALU.add),
                                writes=[("ps", bk), ("H", m, bi)])
                            pump(st["pumpk"])
                        if last:
                            nrm_h[bi] = rmsnorm_async(bi, c0, n, H, "H", list(range(8)), lambda c, l=l: vcol(l, 16, c),
                                                      NRM, "NRM", 1.0 / D)
                    wrel(3)
                dbg(hf, l, 'F')
                psl = [wacq("PG")]
                spp, wpp_ = wacq("PP")
                psl += [wacq("PG"), wacq("PG")]
                for bi, (c0, n, kind) in enumerate(blocks):
                    ensure(nrm_h[bi])
                    for t_ in range(c0 // 128, (c0 + n) // 128):
                        ensure(pt_h[t_])
                    for m in range(8):
                        si, wv = psl[m // 3]
                        mo = (m % 3) * 128
                        bg_, bp = newbank(), newbank()
                        mmgroup(PS[:, bg_, 0:n], bg_,
                                [(wv[:, kc, mo:mo + 128], NRM[:, kc, c0:c0 + n],
                                  [("slot", si), ("NRM", kc, bi)]) for kc in range(8)])
                        mmgroup(PS[:, bp, 0:n], bp,
                                [(wpp_[:, kc, m * 128:(m + 1) * 128], PT[:, kc, c0:c0 + n],
                                  [("slot", spp)] + [("PT", t_) for t_ in range(c0 // 128, (c0 + n) // 128)])
                                 for kc in range(2)])
                        ti = ring("tmp", 3)
                        tmp = TMP[:, ti, 0:n]
                        S.op("act", lambda e, bg_=bg_, tmp=tmp, n=n: e.activation(tmp, PS[:, bg_, 0:n], AF.Sigmoid),
                             writes=[("ps", bg_), ("TMP", ti)])
                        S.op("dve", lambda e, bp=bp, tmp=tmp, n=n: e.tensor_tensor(tmp, tmp, PS[:, bp, 0:n], ALU.mult),
                             writes=[("ps", bp), ("TMP", ti)])
                        S.op("dve", lambda e, tmp=tmp, m=m, c0=c0, n=n: e.tensor_tensor(
                            H[:, m, c0:c0 + n], H[:, m, c0:c0 + n], tmp, ALU.add),
                            reads=[("TMP", ti)], writes=[("H", m, bi)])
                        if bi == nblk - 1 and m == 2:
                            wrel(1)
                        pump(st["pumpk"])
                    if l < DEPTH - 1:
                        nrm_h[bi] = rmsnorm_async(bi, c0, n, H, "H", list(range(8)), lambda c, l=l: vcol(l + 1, 0, c),
                                                  NRM, "NRM", 1.0 / D)
                    else:
                        nrm_h[bi] = rmsnorm_async(bi, c0, n, H, "H", list(range(8)),
                                                  lambda c: VEC[:, DEPTH * NVL + c:DEPTH * NVL + c + 1],
                                                  Y, "Y", 1.0 / D, extra_reads=[("Yreg",)], fence=[("Areg",)])
                wrel(3)
                dbg(hf, l, 'P')

            flush()
            for (col0, row0, _) in tiles:
                bi = blk_of_col(col0)
                si_ = ring("stg", 2)
                for half8 in range(2):
                    bank = newbank()
                    for k4 in range(4):
                        kc = half8 * 4 + k4
                        S.op("pe", lambda e, kc=kc, k4=k4, bank=bank, col0=col0: e.transpose(
                            PS[:, bank, k4 * 128:(k4 + 1) * 128], Y[:, kc, col0:col0 + 128], IDF[:]),
                            reads=[("Y", kc, bi), ("IDF",), ("Yreg",)], writes=[("ps", bank)], inc=(1 if k4 == 3 else 0))
                    if half8 == 0:
                        S.op("act", lambda e, bank=bank, si_=si_: e.activation(STG[:, si_, 0:512], PS[:, bank, :], AF.Copy),
                             writes=[("ps", bank), ("STG", si_)])
                    else:
                        S.op("dve", lambda e, bank=bank, si_=si_: e.tensor_copy(STG[:, si_, 512:1024], PS[:, bank, :]),
                             writes=[("ps", bank), ("STG", si_)])
                dma(y_o[row0:row0 + 128, :], STG[:, si_, :], reads=[("STG", si_)])

        assert _DBG is not None or wst["next_acq"] == len(plan), (wst["next_acq"], len(plan))
        finals = [(f"d{i}", S.cnt.get(f"d{i}", 0)) for i in range(NDS)]

        block = es.enter_context(nc.Block())

        def emit(e, name):
            for (wl, fn, semname, inc) in S.ops[name]:
                for (s, v) in wl:
                    e.wait_ge(SEM[s], v)
                ins = fn(e)
                if inc:
                    ins.then_inc(SEM[semname], inc)

        @block.tensor
        def _(e):
            emit(e, "pe")

        @block.scalar
        def _(e):
            emit(e, "act")

        @block.vector
        def _(e):
            emit(e, "dve")

        @block.gpsimd
        def _(e):
            emit(e, "pool")

        @block.sync
        def _(e):
            emit(e, "sp")
            for (s, v) in finals:
                if v:
                    e.wait_ge(SEM[s], v)
    return nc


_NC_CACHE = {}


def _host_layout(inp):
    f = np.float32
    vec = np.zeros((128, NV), f)
    for l in range(DEPTH):
        o = l * NVL
        vec[:, o + 0:o + 8] = inp["g_mix"][l].reshape(8, 128).T
        vec[:, o + 8:o + 16] = inp["g_ffn"][l].reshape(8, 128).T
        vec[:, o + 16:o + 24] = inp["g_ple"][l].reshape(8, 128).T
        vec[:, o + 24:o + 27] = inp["g_out_a"][l].reshape(3, 128).T
        vec[:, o + 27:o + 30] = inp["g_out_b"][l].reshape(3, 128).T
        vec[:, o + 30:o + 32] = inp["g_out_c"][l].reshape(2, 128).T
        vec[:, o + 32:o + 41] = inp["conv_a_w"][l].reshape(3, 3, 128).transpose(2, 0, 1).reshape(128, 9)
        vec[:, o + 41:o + 103] = inp["conv_c_w"][l].reshape(31, 2, 128).transpose(2, 0, 1).reshape(128, 62)
        vec[:, o + 103:o + 105] = inp["conv_c_b"][l].reshape(2, 128).T
        vec[:, o + 105:o + 107] = inp["ln_c_g"][l].reshape(2, 128).T
        vec[:, o + 107:o + 109] = inp["ln_c_b"][l].reshape(2, 128).T
    vec[:, DEPTH * NVL:DEPTH * NVL + 8] = inp["g_final"].reshape(8, 128).T
    lnb = np.ascontiguousarray(np.broadcast_to(
        np.stack([inp["ln_b_g"], inp["ln_b_b"]], axis=1)[:, None, :, :], (DEPTH, 128, 2, 384))).astype(f)
    ws = inp["w_s"]
    wsT = np.ascontiguousarray(ws.transpose(0, 3, 1, 2))
    sub = ws[:, :, :8, :8]
    wsA = np.ascontiguousarray(np.repeat(sub.transpose(0, 3, 1, 2)[:, :, None, :, :], 16, axis=2)
                               .reshape(DEPTH, 128, 6, 8))
    bs = inp["b_s"].reshape(DEPTH, 3, 2, 128)
    bsP = np.ascontiguousarray(np.repeat(bs.transpose(0, 2, 1, 3)[:, :, None, :, :], 64, axis=2)
                               .reshape(DEPTH, 128, 3, 128))
    bsS = np.ascontiguousarray(np.repeat(bsP[:, :, :, :8, None], 16, axis=4).reshape(DEPTH, 128, 3, 128))
    shared = {"vec": vec, "lnb": lnb, "wsT": wsT, "wsA": wsA, "bsP": bsP, "bsS": bsS}
    for k in ("w_in", "w_out", "w_gate", "w_up", "w_down", "w_ple_gate", "w_ple_proj"):
        shared[k] = np.ascontiguousarray(inp[k], dtype=f)
    maps = []
    for c in range(NCORE):
        b0 = c * 16
        xs = inp["x_sample"][b0:b0 + 16].transpose(1, 0, 2).reshape(128, D)
        xin = np.ascontiguousarray(np.concatenate([inp["x_prompt"][c], xs], axis=0), dtype=f)
        ps_ = inp["p_sample"][:, b0:b0 + 16].transpose(0, 2, 1, 3).reshape(DEPTH, 128, 256)
        pin = np.ascontiguousarray(np.concatenate([inp["p_prompt"][:, c], ps_], axis=1), dtype=f)
        sa = np.ascontiguousarray(inp["state_conv_a"][:, b0:b0 + 16].transpose(0, 2, 1, 3).reshape(DEPTH, 32, 384), dtype=f)
        sc = np.ascontiguousarray(inp["state_conv_c"][:, b0:b0 + 16].transpose(0, 2, 1, 3).reshape(DEPTH, 480, 256), dtype=f)
        m = dict(shared)
        m.update({"xin": xin, "pin": pin, "sa": sa, "sc": sc})
        maps.append(m)
    return maps


def kernel(**inputs):
    inp = {k: np.asarray(v) for k, v in inputs.items()}
    if "nc" not in _NC_CACHE:
        _NC_CACHE["nc"] = build_program()
    nc = _NC_CACHE["nc"]
    maps = _host_layout(inp)
    res = run_bass_kernel_spmd(nc, maps, core_ids=list(range(NCORE)))
    R = res.results
    f = np.float32
    y_prompt = np.stack([np.asarray(R[c]["y"])[:2048] for c in range(NCORE)]).astype(f)
    y_sample = np.concatenate([np.asarray(R[c]["y"])[2048:].reshape(8, 16, D).transpose(1, 0, 2)
                               for c in range(NCORE)], axis=0).astype(f)
    na_p = np.stack([np.asarray(R[c]["nap"]) for c in range(NCORE)], axis=1).astype(f)
    na_s = np.concatenate([np.asarray(R[c]["nas"]).reshape(DEPTH, 2, 16, 384).transpose(0, 2, 1, 3)
                           for c in range(NCORE)], axis=1).astype(f)
    nc_p = np.stack([np.asarray(R[c]["ncp"]) for c in range(NCORE)], axis=1).astype(f)
    nc_s = np.concatenate([np.asarray(R[c]["ncs"]).reshape(DEPTH, 30, 16, 256).transpose(0, 2, 1, 3)
                           for c in range(NCORE)], axis=1).astype(f)
    nv_p = np.stack([np.asarray(R[c]["nvp"]) for c in range(NCORE)], axis=1).astype(f)
    nv_s = np.concatenate([np.asarray(R[c]["nvs"]).reshape(DEPTH, 8, 16, 384).transpose(0, 2, 1, 3)
                           for c in range(NCORE)], axis=1).astype(f)
    return (y_prompt, y_sample, na_p, na_s, nc_p, nc_s, nv_p, nv_s)
```
